# Optimizing a Trainium2 kernel written in Bass

```python
import jax, jax.numpy as jnp
from jax import lax
import numpy as np

D_MODEL = 4096
BATCH = 1
SEQ = 8192
DEPTH = 1

D_MIX = D_MODEL
D_POOL = D_MIX // 2
POOL_WINDOWS = (2, 4, 8, 16)
N_POOL_GROUPS = len(POOL_WINDOWS)
POOL_GROUP = D_POOL // N_POOL_GROUPS
D_GLA = D_MIX - D_POOL
GLA_HEADS = 4
GLA_DV = D_GLA // GLA_HEADS
GLA_DK = GLA_DV // 2
GLA_KEY = GLA_HEADS * GLA_DK
GLA_RANK = 16
GLA_TAU = 16.0
GLA_CHUNK = 64
EPS = 1e-6

IN_SIZES = (D_POOL, D_POOL, GLA_KEY, GLA_KEY, D_GLA, D_GLA, GLA_RANK)
D_IN = sum(IN_SIZES)

kernel_name = "hybrid_pool_gla_adaln_layer"


def _rmsnorm(x, w):
    xf = x.astype(jnp.float32)
    return xf * lax.rsqrt(jnp.mean(xf * xf, axis=-1, keepdims=True) + EPS) * w.astype(jnp.float32)


def _pool_mixer(u, w_pool, pool_scale):
    B, T, _ = u.shape
    ug = u.reshape(B, T, N_POOL_GROUPS, POOL_GROUP)
    cs = jnp.cumsum(ug, axis=1)
    t = jnp.arange(T)
    outs = []
    for g, w in enumerate(POOL_WINDOWS):
        c_g = cs[:, :, g]
        lag = jnp.pad(c_g[:, :T - w], ((0, 0), (w, 0), (0, 0)))
        cnt = jnp.minimum(t + 1, w).astype(jnp.float32)[None, :, None]
        outs.append((c_g - lag) / cnt - ug[:, :, g])
    pooled = jnp.stack(outs, axis=2)
    mixed = jnp.einsum('btgc,gcd->btgd', pooled, w_pool.astype(jnp.float32))
    return mixed.reshape(B, T, D_POOL) * pool_scale.astype(jnp.float32)


def _gla_chunked(q, k, v, log_a):
    B, T, H, dk = q.shape
    dv = v.shape[-1]
    C = GLA_CHUNK
    N = T // C

    def to_chunks(a):
        return a.reshape(B, N, C, H, a.shape[-1]).transpose(1, 0, 3, 2, 4)

    causal = jnp.tril(jnp.ones((C, C), dtype=bool))[:, :, None]

    def step(S, inp):
        qc, kc, vc, gc = inp
        b = jnp.cumsum(gc, axis=2)
        o_inter = jnp.einsum('bhcd,bhde->bhce', qc * jnp.exp(b), S)
        diff = b[:, :, :, None, :] - b[:, :, None, :, :]
        decay = jnp.exp(jnp.where(causal, diff, -jnp.inf))
        A = jnp.einsum('bhid,bhjd,bhijd->bhij', qc, kc, decay)
        o_intra = jnp.einsum('bhij,bhje->bhie', A, vc)
        b_last = b[:, :, -1:, :]
        k_dec = kc * jnp.exp(b_last - b)
        S_new = jnp.exp(b_last[:, :, 0, :])[..., None] * S + jnp.einsum('bhjd,bhje->bhde', k_dec, vc)
        return S_new, o_inter + o_intra

    S0 = jnp.zeros((B, H, dk, dv), jnp.float32)
    _, o = lax.scan(step, S0, (to_chunks(q), to_chunks(k), to_chunks(v), to_chunks(log_a)))
    return o.transpose(1, 0, 3, 2, 4).reshape(B, T, H, dv)


def setup_inputs(seed: int = 0) -> dict:
    key = jax.random.key(seed)
    ks = jax.random.split(key, 14)
    f32 = jnp.float32
    D = D_MODEL
    x = jax.random.normal(ks[0], (BATCH, SEQ, D), f32)
    c = jax.random.normal(ks[1], (BATCH, D), f32)
    w_ada = jax.random.normal(ks[2], (DEPTH, D, 3 * D), f32) * (0.5 * D ** -0.5)
    b_ada = jax.random.normal(ks[3], (DEPTH, 3 * D), f32) * 0.02
    norm_w = 1.0 + 0.02 * jax.random.normal(ks[4], (DEPTH, D), f32)
    w_in = jax.random.normal(ks[5], (DEPTH, D, D_IN), f32) * D ** -0.5
    w_pool = jax.random.normal(ks[6], (DEPTH, N_POOL_GROUPS, POOL_GROUP, POOL_GROUP), f32) * POOL_GROUP ** -0.5
    pool_scale = 1.0 + 0.02 * jax.random.normal(ks[7], (DEPTH, D_POOL), f32)
    w_alpha = jax.random.normal(ks[8], (DEPTH, GLA_RANK, GLA_KEY), f32) * GLA_RANK ** -0.5
    b_alpha = jax.random.normal(ks[9], (DEPTH, GLA_KEY), f32) * 0.02
    gla_norm_w = 1.0 + 0.02 * jax.random.normal(ks[10], (DEPTH, GLA_DV), f32)
    w_out = jax.random.normal(ks[11], (DEPTH, D_MIX, D), f32) * D_MIX ** -0.5
    final_norm_w = 1.0 + 0.02 * jax.random.normal(ks[12], (D,), f32)
    return {"x": x, "c": c, "w_ada": w_ada, "b_ada": b_ada, "norm_w": norm_w,
            "w_in": w_in, "w_pool": w_pool, "pool_scale": pool_scale,
            "w_alpha": w_alpha, "b_alpha": b_alpha, "gla_norm_w": gla_norm_w,
            "w_out": w_out, "final_norm_w": final_norm_w}


def reference(x, c, w_ada, b_ada, norm_w, w_in, w_pool, pool_scale, w_alpha, b_alpha,
              gla_norm_w, w_out, final_norm_w):
    B, T, D = x.shape
    in_dtype = x.dtype
    split_idx = [int(v) for v in np.cumsum(IN_SIZES)[:-1]]
    c_act = jax.nn.silu(c.astype(jnp.float32))
    for l in range(DEPTH):
        mod = c_act @ w_ada[l].astype(jnp.float32) + b_ada[l].astype(jnp.float32)
        shift, scale, gate = jnp.split(mod, 3, axis=-1)
        h = _rmsnorm(x, norm_w[l]) * (1.0 + scale[:, None, :]) + shift[:, None, :]
        z = jnp.einsum('btd,de->bte', h, w_in[l].astype(jnp.float32))
        u, g_pool, q, k, v, g_gla, a_lr = jnp.split(z, split_idx, axis=-1)

        y_pool = _pool_mixer(u, w_pool[l], pool_scale[l]) * jax.nn.silu(g_pool)

        log_a = jax.nn.log_sigmoid(a_lr @ w_alpha[l].astype(jnp.float32)
                                   + b_alpha[l].astype(jnp.float32)) / GLA_TAU
        qh = q.reshape(B, T, GLA_HEADS, GLA_DK) * (GLA_DK ** -0.5)
        kh = k.reshape(B, T, GLA_HEADS, GLA_DK)
        vh = v.reshape(B, T, GLA_HEADS, GLA_DV)
        ah = log_a.reshape(B, T, GLA_HEADS, GLA_DK)
        o = _gla_chunked(qh, kh, vh, ah)
        o = _rmsnorm(o, gla_norm_w[l]).reshape(B, T, D_GLA)
        y_gla = o * jax.nn.silu(g_gla)

        y = jnp.einsum('btm,md->btd', jnp.concatenate([y_pool, y_gla], axis=-1),
                       w_out[l].astype(jnp.float32))
        x = (x.astype(jnp.float32) + gate[:, None, :] * y).astype(in_dtype)
    return _rmsnorm(x, final_norm_w).astype(in_dtype)
```

```python
from contextlib import ExitStack

import numpy as np
import concourse.bass as bass
import concourse.mybir as mybir
from concourse.bass_utils import run_bass_kernel_spmd

F32 = mybir.dt.float32
BF16 = mybir.dt.bfloat16
AF = mybir.ActivationFunctionType
ALU = mybir.AluOpType
AX = mybir.AxisListType

N_CORES = 8
D = 4096
KT = D // 128
D_POOL = 2048
D_GLA = 2048
GLA_HEADS = 4
GLA_DK = 256
GLA_DV = 512
GLA_KEY = 1024
GLA_RANK = 16
GLA_TAU = 16.0
D_IN = 10256
EPS = 1e-6
POOL_WINDOWS = (2, 4, 8, 16)
OFF_U, OFF_GP, OFF_Q, OFF_K, OFF_V, OFF_G, OFF_A = 0, 2048, 4096, 5120, 6144, 8192, 10240
WB = 256

ENGS = ("pe", "act", "dve", "pool", "sp")


class Buf:
    __slots__ = ("name", "last_w", "rd_eng", "rd_dma")

    def __init__(self, name):
        self.name = name
        self.last_w = None
        self.rd_eng = {}
        self.rd_dma = []


class _Op:
    __slots__ = ("id", "eng", "fn", "deps", "marked", "dma", "sem", "val")


class Prog:
    def __init__(self, nc, stack, n_sp=44, n_pool=20):
        self.nc = nc
        self.ops = []
        self.eng_ops = {e: [] for e in ENGS}
        self.last_real = {e: None for e in ENGS}
        self.barrier_deps = {e: set() for e in ENGS}
        self.unconsumed_dma = set()
        self.csem = {e: stack.enter_context(nc.semaphore("c_" + e)) for e in ("pe", "act", "dve", "pool")}
        self.dsem = {
            "sp": [stack.enter_context(nc.semaphore("dsp%d" % i)) for i in range(n_sp)],
            "pool": [stack.enter_context(nc.semaphore("dpl%d" % i)) for i in range(n_pool)],
        }
        self.dsem_rr = {"sp": 0, "pool": 0}
        self.dsem_last = {}

    def op(self, eng, fn, reads=(), writes=(), dma=False):
        o = _Op()
        o.id = len(self.ops)
        o.eng = eng
        o.fn = fn
        o.dma = dma
        o.marked = False
        o.sem = None
        o.val = 0
        deps = set()
        for b in reads:
            if b.last_w is not None:
                deps.add(b.last_w)
        soft = set()
        for b in writes:
            if b.last_w is not None:
                soft.add(b.last_w)
            soft.update(b.rd_eng.values())
            soft.update(b.rd_dma)
        for d in soft:
            od = self.ops[d]
            if (not dma) and (not od.dma) and od.eng == eng:
                continue
            deps.add(d)
        deps.update(self.barrier_deps[eng])
        self.barrier_deps[eng] = set()
        if dma:
            idx = self.dsem_rr[eng]
            self.dsem_rr[eng] = (idx + 1) % len(self.dsem[eng])
            prev = self.dsem_last.get((eng, idx))
            uses = 0
            if prev is not None:
                deps.add(prev[0])
                uses = prev[1]
            o.sem = self.dsem[eng][idx]
            o.val = 16 * (uses + 1)
            self.dsem_last[(eng, idx)] = (o.id, uses + 1)
            self.unconsumed_dma.add(o.id)
        best = {}
        final = []
        for d in deps:
            od = self.ops[d]
            if od.dma:
                final.append(d)
            else:
                if od.eng == "pe" and eng == "pe" and not dma:
                    continue
                if od.eng not in best or best[od.eng] < d:
                    best[od.eng] = d
        final.extend(best.values())
        for d in final:
            od = self.ops[d]
            if od.dma:
                self.unconsumed_dma.discard(d)
            else:
                od.marked = True
        o.deps = sorted(final)
        for b in writes:
            b.last_w = o.id
            b.rd_eng = {}
            b.rd_dma = []
        for b in reads:
            if b.last_w != o.id:
                if dma:
                    b.rd_dma.append(o.id)
                else:
                    b.rd_eng[eng] = o.id
        self.ops.append(o)
        self.eng_ops[eng].append(o)
        if not dma:
            self.last_real[eng] = o.id
        return o.id

    def barrier(self):
        deps = set(self.unconsumed_dma)
        for e in ENGS:
            if e != "sp" and self.last_real[e] is not None:
                deps.add(self.last_real[e])
        for e in ENGS:
            self.barrier_deps[e] |= deps

    def finalize(self):
        for e in ENGS:
            cnt = 0
            for o in self.eng_ops[e]:
                if o.dma:
                    continue
                if o.marked:
                    assert e != "sp"
                    cnt += 1
                    o.sem = self.csem[e]
                    o.val = cnt

    def replay(self, e, eng):
        waited = {}
        for o in self.eng_ops[e]:
            need = {}
            for d in o.deps:
                od = self.ops[d]
                key = od.sem.num
                if waited.get(key, 0) < od.val and need.get(key, (None, 0))[1] < od.val:
                    need[key] = (od.sem, od.val)
            need = list(need.values())
            fuse = need.pop() if (need and e != "sp") else None
            for sem, val in need:
                eng.wait_ge(sem, val)
                waited[sem.num] = val
            ins = o.fn(eng)
            if fuse is not None:
                ins.wait_op(fuse[0], fuse[1], "sem-ge")
                waited[fuse[0].num] = fuse[1]
            if o.dma:
                ins.then_inc(o.sem, 16)
            elif o.marked:
                ins.then_inc(o.sem, 1)

    def run(self):
        self.finalize()
        with self.nc.Block() as block:
            @block.tensor
            def _(eng):
                self.replay("pe", eng)

            @block.scalar
            def _(eng):
                self.replay("act", eng)

            @block.vector
            def _(eng):
                self.replay("dve", eng)

            @block.gpsimd
            def _(eng):
                self.replay("pool", eng)

            @block.sync
            def _(eng):
                self.replay("sp", eng)


class _Rot:
    def __init__(self, items):
        self.items = items
        self.i = 0

    def next(self):
        it = self.items[self.i]
        self.i = (self.i + 1) % len(self.items)
        return it


def build_program(SEQ):
    SEG = SEQ // N_CORES
    NB = N_CORES
    NTB = SEG // 128
    NT = SEQ // 128
    TW = min(256, SEG)
    NHF = SEG // TW
    TPW = TW // 128
    L = SEG + 16

    nc = bass.Bass("TRN2", target_bir_lowering=False)

    def din(name, shape):
        return nc.dram_tensor(name, list(shape), F32, kind="ExternalInput").ap()

    x_d = din("x", [SEQ, D])
    vmask_d = din("vmask", [128, NT])
    vhalo_d = din("vhalo", [128, 16])
    invcnt_d = din("invcnt", [128, 4 * 16])
    ccol_d = din("c_col", [128, KT])
    wada_d = din("w_ada", [D, 3 * D])
    bada_d = din("b_ada_col", [128, 96])
    nw_d = din("nw_col", [128, KT])
    win_d = din("w_in", [D, D_IN])
    wpool_d = din("w_pool", [4, 512, 512])
    pscol_d = din("ps_col", [128, 16])
    walpha_d = din("w_alpha_aug", [17, GLA_KEY])
    gnw_d = din("gnw_bc", [128, GLA_DV])
    wout_d = din("w_out", [D, D])
    fnw_d = din("fnw_bc", [128, D])
    cmat_d = din("cmat", [128, 3 * 128])
    out_d = nc.dram_tensor("out", [SEG, D], F32, kind="ExternalOutput").ap()
    ymd = nc.dram_tensor("ymixT_d", [32, 128, SEG], BF16, kind="Internal").ap()
    ymd_b = [Buf("ymd%d" % i) for i in range(8)]
    out_b = [Buf("out%d" % i) for i in range(NTB)]

    with ExitStack() as st:
        P = Prog(nc, st)

        def sb(name, shape, dt):
            return st.enter_context(nc.sbuf_tensor("sb_" + name, list(shape), dt))

        hT = sb("hT", [128, KT, SEG], BF16)
        hT_tb = [Buf("hT%d" % t) for t in range(NTB)]

        def hTl(hf):
            return hT_tb[hf * TPW:(hf + 1) * TPW]
        Wt = [sb("W%d" % i, [128, KT, WB], BF16) for i in range(3)]
        W_rot = _Rot([(Wt[i], Buf("W%d" % i)) for i in range(3)])
        S = sb("S", [128, GLA_HEADS, 2, GLA_DV], F32)
        S_b = [[Buf("S%d%d" % (h, k)) for k in range(2)] for h in range(GLA_HEADS)]
        PH_WORDS = 15488
        PH = sb("PH", [128, PH_WORDS], F32)
        cm = sb("cmat", [128, 3 * 128], F32)
        cm_b = Buf("cmat")
        ident = cm[:, 0:128]
        tri = cm[:, 128:256]
        ustr = cm[:, 256:384]
        ident_bf = sb("ident_bf", [128, 128], BF16)
        identbf_b = Buf("identbf")
        ones_mat = sb("ones_mat", [128, 128], F32)
        ones_b = Buf("ones")
        eps_t = sb("eps_t", [128, 1], F32)
        eps_b = Buf("eps")
        vmask = sb("vmask", [128, NT], F32)
        vmask_b = Buf("vmask")
        vhalo = sb("vhalo", [128, 16], F32)
        vhalo_b = Buf("vhalo")
        invcnt = sb("invcnt", [128, 64], F32)
        invcnt_b = Buf("invcnt")
        ccol = sb("ccol", [128, KT], F32)
        ccol_b = Buf("ccol")
        cact = sb("cact", [128, KT], F32)
        cact_b = Buf("cact")
        cbf = sb("cbf", [128, KT], BF16)
        cbf_b = Buf("cbf")
        bada = sb("bada", [128, 96], F32)
        bada_b = Buf("bada")
        nwc = sb("nwc", [128, KT], F32)
        nwc_b = Buf("nwc")
        modc = sb("modc", [128, 96], F32)
        modc_b = Buf("modc")
        scol = sb("scol", [128, KT], F32)
        scol_b = Buf("scol")
        pscol = sb("pscol", [128, 16], F32)
        pscol_b = Buf("pscol")
        walpha_t = [sb("walpha%d" % i, [17, GLA_DK], F32) for i in range(2)]
        walpha_rot = _Rot([(walpha_t[i], Buf("walpha%d" % i)) for i in range(2)])
        walpha_pre = {}
        gnw = sb("gnw", [128, GLA_DV], F32)
        gnw_b = Buf("gnw")
        alrT = sb("alrT", [32, SEG], F32)
        alrT_b = Buf("alrT")
        Walr = sb("Walr", [128, KT, GLA_RANK], BF16)
        Walr_b = Buf("Walr")
        eb = sb("eb", [128, NTB, 8], F32)
        eb_b = [[Buf("eb%d_%d" % (t, h)) for h in range(GLA_HEADS)] for t in range(NTB)]
        hTh = sb("hTh", [128, KT, 16], BF16)
        hTh_b = Buf("hTh")
        smalls = sb("smalls", [128, 64], F32)
        small_rot = _Rot([(smalls[:, i:i + 1], Buf("sm%d" % i)) for i in range(64)])
        ssq = sb("ssq", [128, NTB * 16], F32)
        ssq_b = Buf("ssq")
        fin = sb("fin", [128, 3 * NTB], F32)
        fin_b = Buf("fin")

        pall = [st.enter_context(nc.psum_tensor("pall%d" % i, [128, 512], F32)) for i in range(7)]
        pall_rot = _Rot([(pall[i], Buf("pall%d" % i)) for i in range(7)])
        pacc_rot = pall_rot
        paux_rot = pall_rot

        class Carve:
            def __init__(self):
                self.off = 0

            def f32(self, n, shape=None):
                a = PH[:, self.off:self.off + n]
                self.off += n
                assert self.off <= PH_WORDS
                if shape is not None:
                    a = a.rearrange("p (a b) -> p a b", b=shape[-1]) if len(shape) == 2 else a
                return a

            def bf16(self, n, shape=None):
                words = (n + 1) // 2
                a = PH[:, self.off:self.off + words].bitcast(BF16)
                self.off += words
                assert self.off <= PH_WORDS
                if shape is not None and len(shape) == 2:
                    a = a.rearrange("p (a b) -> p a b", b=shape[-1])
                return a

        def dma(q, out, in_, reads=(), writes=()):
            return P.op(q, lambda e: e.dma_start(out=out, in_=in_), reads=reads, writes=writes, dma=True)

        def mm(out, lhsT, rhs, start, stop, reads, writes):
            return P.op("pe", lambda e: e.matmul(out, lhsT=lhsT, rhs=rhs, start=start, stop=stop),
                        reads=reads, writes=writes)

        def act(out, in_, func, reads, writes, bias=None, scale=None, accum_out=None):
            kw = {}
            if bias is not None:
                kw["bias"] = bias
            if scale is not None:
                kw["scale"] = scale
            if accum_out is not None:
                kw["accum_out"] = accum_out
            return P.op("act", lambda e: e.activation(out=out, in_=in_, func=func, **kw),
                        reads=reads, writes=writes)

        def tsc(out, in0, s1, s2, op0, op1, reads, writes):
            if op1 is None:
                return P.op("dve", lambda e: e.tensor_scalar(out=out, in0=in0, scalar1=s1, scalar2=None, op0=op0),
                            reads=reads, writes=writes)
            return P.op("dve", lambda e: e.tensor_scalar(out=out, in0=in0, scalar1=s1, scalar2=s2, op0=op0, op1=op1),
                        reads=reads, writes=writes)

        def stt(out, in0, scalar, in1, op0, op1, reads, writes):
            return P.op("dve", lambda e: e.scalar_tensor_tensor(out=out, in0=in0, scalar=scalar, in1=in1,
                                                                 op0=op0, op1=op1), reads=reads, writes=writes)

        def tt(out, in0, in1, op, reads, writes):
            return P.op("dve", lambda e: e.tensor_tensor(out=out, in0=in0, in1=in1, op=op),
                        reads=reads, writes=writes)

        def vcopy(out, in_, reads, writes):
            return P.op("dve", lambda e: e.tensor_copy(out=out, in_=in_), reads=reads, writes=writes)

        def recip(out, in_, reads, writes):
            return P.op("dve", lambda e: e.reciprocal(out=out, in_=in_), reads=reads, writes=writes)

        def memset(ap, val, writes):
            return P.op("dve", lambda e: e.memset(ap, val), writes=writes)

        def load_w(src2d, c0, width):
            wt, wb = W_rot.next()
            src = src2d.rearrange("(kt p) c -> p kt c", p=128)[:, :, c0:c0 + width]
            dma("pool", wt[:, :, 0:width], src, writes=[wb])
            return wt, wb

        dma("sp", cm[:], cmat_d[:, :], writes=[cm_b])
        dma("sp", vmask[:], vmask_d[:, :], writes=[vmask_b])
        dma("sp", vhalo[:], vhalo_d[:, :], writes=[vhalo_b])
        dma("sp", invcnt[:], invcnt_d[:, :], writes=[invcnt_b])
        dma("sp", ccol[:], ccol_d[:, :], writes=[ccol_b])
        dma("sp", bada[:], bada_d[:, :], writes=[bada_b])
        dma("sp", nwc[:], nw_d[:, :], writes=[nwc_b])
        dma("sp", pscol[:], pscol_d[:, :], writes=[pscol_b])
        dma("sp", gnw[:], gnw_d[:, :], writes=[gnw_b])
        dma("pool", Walr[:], win_d.rearrange("(kt p) c -> p kt c", p=128)[:, :, OFF_A:OFF_A + GLA_RANK],
            writes=[Walr_b])
        memset(ones_mat[:], 1.0, [ones_b])
        memset(eps_t[:], EPS, [eps_b])
        memset(alrT[:], 1.0, [alrT_b])
        memset(S[:], 0.0, [b for hb in S_b for b in hb])
        vcopy(ident_bf[:], ident, [cm_b], [identbf_b])
        ones_col = ones_mat[:, 0:1]

        act(cact[:], ccol[:], AF.Silu, [ccol_b], [cact_b])
        vcopy(cbf[:], cact[:], [cact_b], [cbf_b])
        pm = st.enter_context(nc.psum_tensor("pmod", [128, 512], F32))
        pm_b = Buf("pmod")
        modg_b = Buf("modg")

        def run_tasks(items):
            w_idx = [i for i, it in enumerate(items) if it[0] == "w"]
            loaded = {}
            state = {"nxt": 0}

            def ensure(upto):
                while state["nxt"] < len(w_idx) and state["nxt"] <= upto:
                    it = items[w_idx[state["nxt"]]]
                    loaded[w_idx[state["nxt"]]] = load_w(it[1], it[2], WB)
                    state["nxt"] += 1

            wpos = 0
            ensure(1)
            for i, it in enumerate(items):
                if it[0] == "f":
                    it[1]()
                else:
                    wt, wb = loaded.pop(i)
                    it[3](wt, wb)
                    wpos += 1
                    ensure(wpos + 1)

        def ada_block(blk):
            def f(wt, wb):
                for c in range(WB // 128):
                    ctg = blk * (WB // 128) + c
                    for kt in range(KT):
                        mm(pm[:, ctg:ctg + 1], wt[:, kt, c * 128:(c + 1) * 128], cbf[:, kt:kt + 1],
                           kt == 0, kt == KT - 1, [wb, cbf_b], [pm_b])
            return ("w", wada_d, blk * WB, f)

        shiftc = modc[:, 0:KT]
        gatec = modc[:, 2 * KT:3 * KT]

        def mod_finish_ss():
            tt(modc[:, 0:2 * KT], pm[:, 0:2 * KT], bada[:, 0:2 * KT], ALU.add, [pm_b, bada_b], [modc_b])
            tsc(scol[:], modc[:, KT:2 * KT], 1.0, None, ALU.add, None, [modc_b], [scol_b])
            tt(scol[:], scol[:], nwc[:], ALU.mult, [scol_b, nwc_b], [scol_b])

        def mod_finish_gate():
            tt(gatec, pm[:, 2 * KT:3 * KT], bada[:, 2 * KT:3 * KT], ALU.add, [pm_b, bada_b], [modg_b])

        n_ss_blk = (2 * D) // WB
        n_ada_blk = (3 * D) // WB
        items = [ada_block(blk) for blk in range(n_ss_blk)]
        items.append(("f", mod_finish_ss))
        gate_blks = list(range(n_ss_blk, n_ada_blk))

        cv = Carve()
        xst = [(cv.f32(4096), Buf("xst0")), (cv.f32(4096), Buf("xst1"))]
        junk = cv.bf16(4096)
        junk_b = Buf("junk")

        def x_load(tt_):
            xt, xb = xst[tt_ % 2]
            dma("sp", xt, x_d[tt_ * 128:(tt_ + 1) * 128, :], writes=[xb])

        dgs = [(cv.f32(128), Buf("dg0")), (cv.f32(128), Buf("dg1"))]
        ph_mark = cv.off

        def h_stats(tt_):
            xt, xb = xst[tt_ % 2]
            dgt, dg_b = dgs[tt_ % 2]
            ss, ss_b = small_rot.next()
            rt, rt_b = small_rot.next()
            rs, rs_b = small_rot.next()
            act(junk, xt, AF.Square, [xb], [junk_b, ss_b], accum_out=ss)
            act(rt, ss, AF.Sqrt, [ss_b, eps_b], [rt_b], bias=eps_t[:], scale=1.0 / D)
            recip(rs, rt, [rt_b], [rs_b])
            tsc(dgt, ident, rs, None, ALU.mult, None, [cm_b, rs_b], [dg_b])

        def h_trans(tt_, tloc):
            xt, xb = xst[tt_ % 2]
            dgt, dg_b = dgs[tt_ % 2]
            for g4 in range(KT // 4):
                pt, pt_b = paux_rot.next()
                for j in range(4):
                    kt = g4 * 4 + j
                    mm(pt[:, j * 128:(j + 1) * 128], xt[:, kt * 128:(kt + 1) * 128], dgt, True, True,
                       [xb, dg_b], [pt_b])
                for j in range(4):
                    kt = g4 * 4 + j
                    if g4 % 4 == 1:
                        act(hT[:, kt, tloc * 128:(tloc + 1) * 128], pt[:, j * 128:(j + 1) * 128], AF.Identity,
                            [pt_b, scol_b, modc_b], [hT_tb[tloc]], bias=shiftc[:, kt:kt + 1],
                            scale=scol[:, kt:kt + 1])
                    else:
                        tsc(hT[:, kt, tloc * 128:(tloc + 1) * 128], pt[:, j * 128:(j + 1) * 128],
                            scol[:, kt:kt + 1], shiftc[:, kt:kt + 1], ALU.mult, ALU.add,
                            [pt_b, scol_b, modc_b], [hT_tb[tloc]])

        def alr_block():
            for hf in range(NHF):
                pa, pa_b = paux_rot.next()
                for kt in range(KT):
                    mm(pa[0:GLA_RANK, 0:TW], Walr[:, kt, :], hT[:, kt, hf * TW:(hf + 1) * TW],
                       kt == 0, kt == KT - 1, [Walr_b] + hTl(hf), [pa_b])
                act(alrT[0:GLA_RANK, hf * TW:(hf + 1) * TW], pa[0:GLA_RANK, 0:TW], AF.Copy, [pa_b], [alrT_b])

        def k_tm_pass(hd, blk_base_tile, wt, wb, kd, kd_b, tmp, la_keep=None, pre_tile=None, own_mode=False):
            pend_t = None
            if hd not in walpha_pre:
                wa_ = walpha_rot.next()
                dma("sp", wa_[0][:], walpha_d[:, hd * GLA_DK:(hd + 1) * GLA_DK], writes=[wa_[1]])
                walpha_pre[hd] = wa_
            walpha, walpha_b = walpha_pre.pop(hd)
            nh = (hd + 1) % GLA_HEADS
            wa_ = walpha_rot.next()
            dma("sp", wa_[0][:], walpha_d[:, nh * GLA_DK:(nh + 1) * GLA_DK], writes=[wa_[1]])
            walpha_pre[nh] = wa_

            def second_half(t, la_sp, la_sp_b):
                pu, pu_b = paux_rot.next()
                if own_mode:
                    for k2 in range(2):
                        mm(pu[:, 256 + k2:257 + k2], la_sp[:, k2 * 128:(k2 + 1) * 128], ones_col, True, True,
                           [la_sp_b, ones_b], [pu_b])
                    act(eb[:, t, hd * 2:hd * 2 + 2], pu[:, 256:258], AF.Exp, [pu_b], [eb_b[t][hd]],
                        scale=-1.0 / GLA_TAU)
                    return
                mm(pu[:, 0:256], ustr, la_sp, True, True, [cm_b, la_sp_b], [pu_b])
                for k2 in range(2):
                    mm(pu[:, 256 + k2:257 + k2], la_sp[:, k2 * 128:(k2 + 1) * 128], ones_col, True, True,
                       [la_sp_b, ones_b], [pu_b])
                dk, dk_b = tmp["dk"]
                act(dk, pu[:, 0:256], AF.Exp, [pu_b], [dk_b], scale=-1.0 / GLA_TAU)
                act(eb[:, t, hd * 2:hd * 2 + 2], pu[:, 256:258], AF.Exp, [pu_b], [eb_b[t][hd]],
                    scale=-1.0 / GLA_TAU)
                pk, pk_b = tmp["pk"][t]
                stt(kd[:, t, :], pk[:, 0:256], vmask[:, blk_base_tile + t:blk_base_tile + t + 1], dk,
                    ALU.mult, ALU.mult, [pk_b, vmask_b, dk_b], [kd_b])

            tmp["pk"] = {}
            for t in range(NTB):
                if pre_tile is not None:
                    pre_tile(t)
                if not own_mode:
                    pk, pk_b = pacc_rot.next()
                    tmp["pk"][t] = (pk, pk_b)
                    for kt in range(KT):
                        mm(pk[:, 0:256], hT[:, kt, t * 128:(t + 1) * 128], wt[:, kt, 0:256],
                           kt == 0, kt == KT - 1, [hT_tb[t], wb], [pk_b])
                pl, pl_b = paux_rot.next()
                mm(pl[:, 0:256], alrT[0:17, t * 128:(t + 1) * 128], walpha[0:17, 0:GLA_DK],
                   True, True, [alrT_b, walpha_b], [pl_b])
                la_e, la_e_b = tmp["la_e"]
                if la_keep is not None:
                    la_sp, la_sp_b = la_keep[0][:, t, :], la_keep[1][t]
                else:
                    la_sp, la_sp_b = tmp["la_sp"][t % 2]
                act(la_e, pl[:, 0:256], AF.Exp, [pl_b], [la_e_b], scale=-1.0)
                act(la_sp, la_e, AF.Ln, [la_e_b, ones_b], [la_sp_b], bias=ones_col)
                if pend_t is not None:
                    second_half(*pend_t)
                pend_t = (t, la_sp, la_sp_b)
            second_half(*pend_t)

        def v_tm_pass(half, wt, wb, v, v_b):
            for t in range(NTB):
                pv, pv_b = pacc_rot.next()
                for kt in range(KT):
                    mm(pv[:, 0:256], hT[:, kt, t * 128:(t + 1) * 128], wt[:, kt, 0:256],
                       kt == 0, kt == KT - 1, [hT_tb[t], wb], [pv_b])
                act(v[:, t, half * 256:(half + 1) * 256], pv[:, 0:256], AF.Copy, [pv_b], [v_b[t]])

        def state_update(hd, t, kd, kd_b, v, v_b, sbf=None):
            for k2 in range(2):
                ps_, ps_b = pacc_rot.next()
                mm(ps_[:, :], kd[:, t, k2 * 128:(k2 + 1) * 128], v[:, t, :], True, True,
                   [kd_b, v_b[t]], [ps_b])
                stt(S[:, hd, k2, :], S[:, hd, k2, :], eb[:, t, hd * 2 + k2:hd * 2 + k2 + 1], ps_[:, :],
                    ALU.mult, ALU.add, [S_b[hd][k2], eb_b[t][hd], ps_b], [S_b[hd][k2]])
                if sbf is not None:
                    act(sbf[0][:, hd, k2, :], S[:, hd, k2, :], AF.Copy, [S_b[hd][k2]], [sbf[1][hd][k2]])

        kd_p = cv.bf16(NTB * 256, (NTB, 256))
        kd_p_b = Buf("kd_p")
        v_p = cv.bf16(NTB * 512, (NTB, 512))
        v_p_b = [Buf("v_p%d" % t) for t in range(NTB)]
        tmp_p = {
            "la_e": (cv.f32(256), Buf("la_e")),
            "la_sp": [(cv.f32(256), Buf("la_sp0")), (cv.f32(256), Buf("la_sp1"))],
            "dk": (cv.f32(256), Buf("dk")),
        }

        def h_phase0():
            def f():
                x_load(0)
                x_load(1)
                h_stats(0)
                for t in range(NTB):
                    if t + 1 < NTB:
                        h_stats(t + 1)
                    h_trans(t, t)
                    if t + 2 < NT:
                        x_load(t + 2)
                alr_block()
            return ("f", f)

        def k_task(b, hd):
            def f(wt, wb):
                k_tm_pass(hd, b * NTB, wt, wb, kd_p, kd_p_b, tmp_p)
            return ("w", win_d, OFF_K + hd * 256, f)

        def v_task(b, hd, half):
            def f(wt, wb):
                if half == 0:
                    v_tm_pass(0, wt, wb, v_p, v_p_b)
                    return
                for t in range(NTB):
                    pv, pv_b = pacc_rot.next()
                    for kt in range(KT):
                        mm(pv[:, 0:256], hT[:, kt, t * 128:(t + 1) * 128], wt[:, kt, 0:256],
                           kt == 0, kt == KT - 1, [hT_tb[t], wb], [pv_b])
                    act(v_p[:, t, 256:512], pv[:, 0:256], AF.Copy, [pv_b], [v_p_b[t]])
                    if t > 0:
                        state_update(hd, t - 1, kd_p, kd_p_b, v_p, v_p_b)
                state_update(hd, NTB - 1, kd_p, kd_p_b, v_p, v_p_b)
            return ("w", win_d, OFF_V + hd * 512 + half * 256, f)

        held = []

        def v_hold(b, hd):
            def f(wt, wb):
                held.append((wt, wb))
            return ("w", win_d, OFF_V + hd * 512, f)

        def v_last(b, hd):
            def f(wt2, wb2):
                wt1, wb1 = held.pop()
                if b + 1 == NB - 1:
                    vcopy(hTh[:], hT[:, :, SEG - 16:SEG], [hT_tb[NTB - 1]], [hTh_b])
                h_stats((b + 1) * NTB)
                for t in range(NTB):
                    for half, (wt, wb) in enumerate(((wt1, wb1), (wt2, wb2))):
                        pv, pv_b = pacc_rot.next()
                        for kt in range(KT):
                            mm(pv[:, 0:256], hT[:, kt, t * 128:(t + 1) * 128], wt[:, kt, 0:256],
                               kt == 0, kt == KT - 1, [hT_tb[t], wb], [pv_b])
                        act(v_p[:, t, half * 256:(half + 1) * 256], pv[:, 0:256], AF.Copy, [pv_b], [v_p_b[t]])
                    tt_ = (b + 1) * NTB + t
                    if t + 1 < NTB:
                        h_stats(tt_ + 1)
                    if t > 0:
                        state_update(hd, t - 1, kd_p, kd_p_b, v_p, v_p_b)
                    h_trans(tt_, t)
                    if tt_ + 2 < NT:
                        x_load(tt_ + 2)
                state_update(hd, NTB - 1, kd_p, kd_p_b, v_p, v_p_b)
                if b + 1 < NB - 1:
                    alr_block()
            return ("w", win_d, OFF_V + hd * 512 + 256, f)

        items.append(h_phase0())
        for b in range(NB - 1):
            for hd in range(GLA_HEADS):
                items.append(k_task(b, hd))
                if hd < GLA_HEADS - 1:
                    items.append(v_task(b, hd, 0))
                    items.append(v_task(b, hd, 1))
                else:
                    items.append(v_hold(b, hd))
                    items.append(v_last(b, hd))
                if gate_blks:
                    items.append(ada_block(gate_blks.pop(0)))
                    if not gate_blks:
                        items.append(("f", mod_finish_gate))
        assert not gate_blks
        run_tasks(items)

        pre_gla = [load_w(win_d, OFF_K, 256), load_w(win_d, OFF_Q, 256)]
        P.barrier()
        cv = Carve()
        kd_o = cv.bf16(NTB * 256, (NTB, 256))
        kd_o_b = Buf("kd_o")
        v_o = cv.bf16(NTB * 512, (NTB, 512))
        v_o_b = [Buf("v_o%d" % t) for t in range(NTB)]
        keT = cv.bf16(2 * SEG, (2, SEG))
        keT_b = Buf("keT")
        qeT = cv.bf16(2 * SEG, (2, SEG))
        qeT_b = Buf("qeT")
        gp = cv.f32(NTB * 512, (NTB, 512))
        gp_b = [Buf("gp%d" % t) for t in range(NTB)]
        lasp = cv.f32(NTB * 256, (NTB, 256))
        lasp_b = [Buf("lasp%d" % t) for t in range(NTB)]
        Sbf = cv.bf16(GLA_HEADS * 2 * 512).rearrange("p (h k e) -> p h k e", h=GLA_HEADS, k=2)
        Sbf_b = [[Buf("Sbf%d%d" % (h, k)) for k in range(2)] for h in range(GLA_HEADS)]
        tmp_o = {"la_e": (cv.f32(256), Buf("la_e_o"))}
        kdTs = [(cv.bf16(TW), Buf("kdT0")), (cv.bf16(TW), Buf("kdT1"))]
        kd_pending = []
        kd_i = [0]
        eqt = (cv.f32(512), Buf("eqt"))
        ekt = (cv.f32(512), Buf("ekt"))
        gs = (eqt[0][:, 0:256], eqt[1])
        junk_o = (ekt[0].bitcast(BF16)[:, 0:512], ekt[1])
        ATbf = (cv.bf16(128), Buf("ATbf"))
        y_t = (cv.bf16(512), Buf("y_t"))
        yT_sb = (cv.bf16(512, (4, 128)), Buf("yT_sb"))

        alr_block()
        for hd in range(GLA_HEADS):
            for k2 in range(2):
                act(Sbf[:, hd, k2, :], S[:, hd, k2, :], AF.Copy, [S_b[hd][k2]], [Sbf_b[hd][k2]])

        own_tile0 = (NB - 1) * NTB
        for hd in range(GLA_HEADS):
            if hd == 0:
                wk, wq = pre_gla
            else:
                wk = load_w(win_d, OFF_K + hd * 256, 256)
                wq = load_w(win_d, OFF_Q + hd * 256, 256)
            k_tm_pass(hd, own_tile0, wk[0], wk[1], kd_o, kd_o_b, tmp_o, la_keep=(lasp, lasp_b), own_mode=True)
            wv0 = load_w(win_d, OFF_V + hd * 512, 256)
            for k2 in range(2):
                for hf in range(NHF):
                    pbT, pbT_b = paux_rot.next()
                    for j in range(TPW):
                        t = hf * TPW + j
                        mm(pbT[:, j * 128:(j + 1) * 128], lasp[:, t, k2 * 128:(k2 + 1) * 128], tri, True, True,
                           [lasp_b[t], cm_b], [pbT_b])
                    act(eqt[0][:, 0:TW], pbT[:, 0:TW], AF.Exp, [pbT_b], [eqt[1]], scale=-1.0 / GLA_TAU)
                    act(ekt[0][:, 0:TW], pbT[:, 0:TW], AF.Exp, [pbT_b], [ekt[1]], scale=1.0 / GLA_TAU)
                    pq, pq_b = pacc_rot.next()
                    for kt in range(KT):
                        mm(pq[:, 0:TW], wq[0][:, kt, k2 * 128:(k2 + 1) * 128], hT[:, kt, hf * TW:(hf + 1) * TW],
                           kt == 0, kt == KT - 1, [wq[1]] + hTl(hf), [pq_b])
                    pk2, pk2_b = pacc_rot.next()
                    for kt in range(KT):
                        mm(pk2[:, 0:TW], wk[0][:, kt, k2 * 128:(k2 + 1) * 128], hT[:, kt, hf * TW:(hf + 1) * TW],
                           kt == 0, kt == KT - 1, [wk[1]] + hTl(hf), [pk2_b])
                    while kd_pending:
                        kd_pending.pop(0)()
                    stt(qeT[:, k2, hf * TW:(hf + 1) * TW], pq[:, 0:TW], GLA_DK ** -0.5, eqt[0][:, 0:TW],
                        ALU.mult, ALU.mult, [pq_b, eqt[1]], [qeT_b])
                    tt(keT[:, k2, hf * TW:(hf + 1) * TW], pk2[:, 0:TW], ekt[0][:, 0:TW], ALU.mult,
                       [pk2_b, ekt[1]], [keT_b])
                    kdT_t, kdT_b = kdTs[kd_i[0] % 2]
                    kd_i[0] += 1
                    for j in range(TPW):
                        t = hf * TPW + j
                        stt(kdT_t[:, j * 128:(j + 1) * 128], pk2[:, j * 128:(j + 1) * 128],
                            eb[:, t, hd * 2 + k2:hd * 2 + k2 + 1], ekt[0][:, j * 128:(j + 1) * 128],
                            ALU.mult, ALU.mult, [pk2_b, eb_b[t][hd], ekt[1]], [kdT_b])

                    def flush(k2=k2, hf=hf, kdT_t=kdT_t, kdT_b=kdT_b):
                        pkd_t, pkd_b = pall_rot.next()
                        pkd = pkd_t[:, :].bitcast(BF16)
                        for j in range(TPW):
                            P.op("pe", lambda e, o=pkd[:, j * 128:(j + 1) * 128], i=kdT_t[:, j * 128:(j + 1) * 128]:
                                 e.transpose(out=o, in_=i, identity=ident_bf[:]), reads=[kdT_b, identbf_b],
                                 writes=[pkd_b])
                        for j in range(TPW):
                            t = hf * TPW + j
                            act(kd_o[:, t, k2 * 128:(k2 + 1) * 128], pkd[:, j * 128:(j + 1) * 128], AF.Copy,
                                [pkd_b], [kd_o_b])
                    kd_pending.append(flush)
            while kd_pending:
                kd_pending.pop(0)()
            wv1 = load_w(win_d, OFF_V + hd * 512 + 256, 256)
            v_tm_pass(0, wv0[0], wv0[1], v_o, v_o_b)
            wg0 = load_w(win_d, OFF_G + hd * 512, 256)
            v_tm_pass(1, wv1[0], wv1[1], v_o, v_o_b)
            wg1 = load_w(win_d, OFF_G + hd * 512 + 256, 256)
            def emit_yT(t, hd=hd):
                pbf_t, pbf_b = pall_rot.next()
                pbf = pbf_t[:, :].bitcast(BF16)
                for j in range(4):
                    P.op("pe", lambda e, o=pbf[:, j * 128:(j + 1) * 128], i=y_t[0][:, j * 128:(j + 1) * 128]:
                         e.transpose(out=o, in_=i, identity=ident_bf[:]), reads=[y_t[1], identbf_b],
                         writes=[pbf_b])
                act(yT_sb[0], pbf[:, 0:512].rearrange("p (a b) -> p a b", b=128), AF.Copy, [pbf_b], [yT_sb[1]])
                mt0 = 16 + hd * 4
                dma("sp", ymd[mt0:mt0 + 4, :, t * 128:(t + 1) * 128].rearrange("j p c -> p j c"), yT_sb[0],
                    reads=[yT_sb[1]], writes=[ymd_b[4 + hd]])

            for t in range(NTB):
                pA, pA_b = paux_rot.next()
                for k2 in range(2):
                    mm(pA[:, 0:128], keT[:, k2, t * 128:(t + 1) * 128], qeT[:, k2, t * 128:(t + 1) * 128],
                       k2 == 0, k2 == 1, [keT_b, qeT_b], [pA_b])
                tt(ATbf[0], pA[:, 0:128], tri, ALU.mult, [pA_b, cm_b], [ATbf[1]])
                for half, wg in ((0, wg0), (1, wg1)):
                    pg, pg_b = pacc_rot.next()
                    for kt in range(KT):
                        mm(pg[:, 0:256], hT[:, kt, t * 128:(t + 1) * 128], wg[0][:, kt, 0:256],
                           kt == 0, kt == KT - 1, [hT_tb[t], wg[1]], [pg_b])
                    act(gs[0], pg[:, 0:256], AF.Silu, [pg_b], [gs[1]])
                    tt(gp[:, t, half * 256:(half + 1) * 256], gs[0], gnw[:, half * 256:(half + 1) * 256], ALU.mult,
                       [gs[1], gnw_b], [gp_b[t]])
                if t > 0:
                    emit_yT(t - 1)
                po, po_b = pacc_rot.next()
                for k2 in range(2):
                    mm(po[:, :], qeT[:, k2, t * 128:(t + 1) * 128], Sbf[:, hd, k2, :], k2 == 0, False,
                       [qeT_b, Sbf_b[hd][k2]], [po_b])
                mm(po[:, :], ATbf[0], v_o[:, t, :], False, True, [ATbf[1], v_o_b[t]], [po_b])
                so, so_b = small_rot.next()
                ro, ro_b = small_rot.next()
                rso, rso_b = small_rot.next()
                act(junk_o[0], po[:, :], AF.Square, [po_b], [junk_o[1], so_b], accum_out=so)
                act(ro, so, AF.Sqrt, [so_b, eps_b], [ro_b], bias=eps_t[:], scale=1.0 / GLA_DV)
                recip(rso, ro, [ro_b], [rso_b])
                stt(y_t[0], po[:, :], rso, gp[:, t, :], ALU.mult, ALU.mult, [po_b, rso_b, gp_b[t]], [y_t[1]])
                state_update(hd, t, kd_o, kd_o_b, v_o, v_o_b, sbf=(Sbf, Sbf_b))
            emit_yT(NTB - 1)

        pre_pool = [load_w(win_d, OFF_U, 256), load_w(win_d, OFF_U + 256, 256)]
        P.barrier()
        cv = Carve()
        u_t = (cv.f32(L), Buf("u_t"))
        pa_t = (cv.f32(L), Buf("pa_t"))
        pb_t = (cv.f32(L), Buf("pb_t"))
        t16 = (cv.f32(16), Buf("t16"))
        pooled = cv.bf16(4 * SEG, (4, SEG))
        pooled_b = [Buf("pooled%d" % c) for c in range(4)]
        gps_all = cv.f32(4 * SEG, (4, SEG))
        gps_all_b = [Buf("gps%d" % i) for i in range(4)]
        ypT = cv.bf16(4 * SEG, (4, SEG))
        ypT_b = Buf("ypT")
        wp_all = cv.bf16(16 * 512).rearrange("p (g c d) -> p g c d", g=4, c=4)
        wp_b = Buf("wp")
        dma("pool", wp_all, wpool_d.rearrange("g (ct p) d -> p g ct d", p=128), writes=[wp_b])

        for g in range(4):
            w = POOL_WINDOWS[g]
            wp = wp_all[:, g, :, :]
            for blk in range(2):
                if g == 0:
                    wu = pre_pool[blk]
                else:
                    wu = load_w(win_d, OFF_U + g * 512 + blk * 256, 256)
                for c in range(2):
                    ct = blk * 2 + c
                    for hf in range(NHF):
                        pu, pu_b = pacc_rot.next()
                        for kt in range(KT):
                            mm(pu[:, 0:TW], wu[0][:, kt, c * 128:(c + 1) * 128], hT[:, kt, hf * TW:(hf + 1) * TW],
                               kt == 0, kt == KT - 1, [wu[1]] + hTl(hf), [pu_b])
                        act(u_t[0][:, 16 + hf * TW:16 + (hf + 1) * TW], pu[:, 0:TW], AF.Copy, [pu_b], [u_t[1]])
                    ph_, ph_b = paux_rot.next()
                    for kt in range(KT):
                        mm(ph_[:, 0:16], wu[0][:, kt, c * 128:(c + 1) * 128], hTh[:, kt, :],
                           kt == 0, kt == KT - 1, [wu[1], hTh_b], [ph_b])
                    tt(u_t[0][:, 0:16], ph_[:, 0:16], vhalo[:], ALU.mult, [ph_b, vhalo_b], [u_t[1]])
                    src, dst = u_t, pa_t
                    sh = 1
                    other = pb_t
                    for step in range(g + 1):
                        lo = 2 * sh - 1
                        tt(dst[0][:, lo:L], src[0][:, lo:L], src[0][:, lo - sh:L - sh], ALU.add,
                           [src[1]], [dst[1]])
                        src = dst
                        dst, other = other, dst
                        sh *= 2
                    win = src
                    stt(pooled[:, ct, :], win[0][:, 16:L], 1.0 / w, u_t[0][:, 16:L], ALU.mult, ALU.subtract,
                        [win[1], u_t[1]], [pooled_b[ct]])
                    tt(t16[0], win[0][:, 16:32], invcnt[:, g * 16:(g + 1) * 16], ALU.mult,
                       [win[1], invcnt_b], [t16[1]])
                    tt(pooled[:, ct, 0:16], t16[0], u_t[0][:, 16:32], ALU.subtract, [t16[1], u_t[1]],
                       [pooled_b[ct]])
            for blk in range(2):
                wg_ = load_w(win_d, OFF_GP + g * 512 + blk * 256, 256)
                for c in range(2):
                    dt_ = blk * 2 + c
                    for hf in range(NHF):
                        pg, pg_b = pacc_rot.next()
                        for kt in range(KT):
                            mm(pg[:, 0:TW], wg_[0][:, kt, c * 128:(c + 1) * 128], hT[:, kt, hf * TW:(hf + 1) * TW],
                               kt == 0, kt == KT - 1, [wg_[1]] + hTl(hf), [pg_b])
                        act(gps_all[:, dt_, hf * TW:(hf + 1) * TW], pg[:, 0:TW], AF.Silu, [pg_b], [gps_all_b[dt_]])
            for dt_ in range(4):
                for hf in range(NHF):
                    pm_, pm_b2 = pacc_rot.next()
                    for ct in range(4):
                        mm(pm_[:, 0:TW], wp[:, ct, dt_ * 128:(dt_ + 1) * 128], pooled[:, ct, hf * TW:(hf + 1) * TW],
                           ct == 0, ct == 3, [wp_b, pooled_b[ct]], [pm_b2])
                    stt(ypT[:, dt_, hf * TW:(hf + 1) * TW], pm_[:, 0:TW], pscol[:, g * 4 + dt_:g * 4 + dt_ + 1],
                        gps_all[:, dt_, hf * TW:(hf + 1) * TW], ALU.mult, ALU.mult,
                        [pm_b2, pscol_b, gps_all_b[dt_]], [ypT_b])
            dma("sp", ymd[g * 4:(g + 1) * 4, :, :].rearrange("j p c -> p j c"), ypT, reads=[ypT_b],
                writes=[ymd_b[g]])

        pre_out = [load_w(wout_d, 0, WB), load_w(wout_d, WB, WB)]
        P.barrier()
        cv = Carve()
        xa = []
        ra = []
        for i in range(2):
            xa.append((cv.f32(NTB * WB, (NTB, WB)), Buf("xa%d" % i)))
            ra.append((cv.f32(NTB * WB, (NTB, WB)), Buf("ra%d" % i)))
        assert cv.off <= 8192
        cv.off = 8192
        rfull = [(PH[:, 0:4096], Buf("rfull0")), (PH[:, 4096:8192], Buf("rfull1"))]
        gbc = (cv.f32(4096), Buf("gbc"))
        dg = (cv.f32(128), Buf("dg"))
        junk2 = (cv.bf16(256), Buf("junk2"))
        out_cb = [Buf("outcb%d" % i) for i in range(D // WB)]

        ymT = hT
        dma("sp", ymT[:], ymd.rearrange("m p c -> p m c"), reads=ymd_b, writes=hT_tb)
        for g4 in range(KT // 4):
            pgb, pgb_b = paux_rot.next()
            for j in range(4):
                kt = g4 * 4 + j
                tsc(dg[0], ident, gatec[:, kt:kt + 1], None, ALU.mult, None, [cm_b, modg_b], [dg[1]])
                mm(pgb[:, j * 128:(j + 1) * 128], ones_mat[:], dg[0], True, True, [ones_b, dg[1]], [pgb_b])
            vcopy(gbc[0][:, g4 * 512:(g4 + 1) * 512], pgb[:, :], [pgb_b], [gbc[1]])

        n_out_blk = D // WB
        row_own = own_tile0 * 128

        def xa_load(cb):
            dma("sp", xa[cb % 2][0],
                x_d[row_own:row_own + SEG, cb * WB:(cb + 1) * WB].rearrange("(t p) c -> p t c", p=128),
                writes=[xa[cb % 2][1]])

        pend = pre_out
        xa_load(0)
        for cb in range(n_out_blk):
            wo = pend.pop(0)
            if cb + 1 < n_out_blk:
                xa_load(cb + 1)
            xa_, xa_b = xa[cb % 2]
            ra_, ra_b = ra[cb % 2]
            for t in range(NTB):
                py, py_b = pacc_rot.next()
                for kt in range(KT):
                    mm(py[:, 0:WB], ymT[:, kt, t * 128:(t + 1) * 128], wo[0][:, kt, :],
                       kt == 0, kt == KT - 1, [hT_tb[t], wo[1]], [py_b])
                tt(ra_[:, t, :], py[:, 0:WB], gbc[0][:, cb * WB:(cb + 1) * WB], ALU.mult, [py_b, gbc[1]], [ra_b])
                tt(ra_[:, t, :], ra_[:, t, :], xa_[:, t, :], ALU.add, [ra_b, xa_b], [ra_b])
                act(junk2[0], ra_[:, t, :], AF.Square, [ra_b], [junk2[1], ssq_b],
                    accum_out=ssq[:, t * 16 + cb:t * 16 + cb + 1])
            dma("sp", out_d[:, cb * WB:(cb + 1) * WB].rearrange("(t p) c -> p t c", p=128), ra_,
                reads=[ra_b], writes=[out_cb[cb]])
            if cb + 2 < n_out_blk:
                pend.append(load_w(wout_d, (cb + 2) * WB, WB))

        P.op("dve", lambda e: e.tensor_reduce(out=fin[:, 0:NTB], in_=ssq[:].rearrange("p (t c) -> p t c", c=16),
                                              axis=AX.X, op=ALU.add), reads=[ssq_b], writes=[fin_b])
        act(fin[:, NTB:2 * NTB], fin[:, 0:NTB], AF.Sqrt, [fin_b, eps_b], [fin_b], bias=eps_t[:], scale=1.0 / D)
        recip(fin[:, 2 * NTB:3 * NTB], fin[:, NTB:2 * NTB], [fin_b], [fin_b])
        dma("sp", gbc[0], fnw_d[:, :], writes=[gbc[1]])
        P.barrier()
        for t in range(NTB):
            rf, rf_b = rfull[t % 2]
            dma("sp", rf, out_d[t * 128:(t + 1) * 128, :], reads=out_cb, writes=[rf_b])
            stt(rf, rf, fin[:, 2 * NTB + t:2 * NTB + t + 1], gbc[0], ALU.mult, ALU.mult,
                [rf_b, fin_b, gbc[1]], [rf_b])
            dma("sp", out_d[t * 128:(t + 1) * 128, :], rf, reads=[rf_b], writes=[out_b[t]])

        P.barrier()
        P.op("sp", lambda e: e.nop())
        import os
        if os.environ.get("KDBG"):
            print("sbuf remaining", nc.sbuf_bytes_remaining if not callable(nc.sbuf_bytes_remaining) else nc.sbuf_bytes_remaining())
        P.run()
    return nc


def _const_mats():
    j = np.arange(128)[:, None]
    i = np.arange(128)[None, :]
    ident = (j == i).astype(np.float32)
    tri = (j <= i).astype(np.float32)
    ustr = (j > i).astype(np.float32)
    return np.ascontiguousarray(np.concatenate([ident, tri, ustr], axis=1))


def _col(v):
    v = np.asarray(v, dtype=np.float32).reshape(-1, 128)
    return np.ascontiguousarray(v.T)


_NC_CACHE = {}


def kernel(x, c, w_ada, b_ada, norm_w, w_in, w_pool, pool_scale, w_alpha, b_alpha,
           gla_norm_w, w_out, final_norm_w):
    x = np.asarray(x, dtype=np.float32)
    B, SEQ, d = x.shape
    assert B == 1 and d == D and SEQ % (N_CORES * 128) == 0
    SEG = SEQ // N_CORES
    NT = SEQ // 128
    if SEQ not in _NC_CACHE:
        _NC_CACHE[SEQ] = build_program(SEQ)
    nc = _NC_CACHE[SEQ]

    x2 = x[0]
    shared = {
        "c_col": _col(np.asarray(c, np.float32)[0]),
        "w_ada": np.ascontiguousarray(np.asarray(w_ada, np.float32)[0]),
        "b_ada_col": _col(np.asarray(b_ada, np.float32)[0]),
        "nw_col": _col(np.asarray(norm_w, np.float32)[0]),
        "w_in": np.ascontiguousarray(np.asarray(w_in, np.float32)[0]),
        "w_pool": np.ascontiguousarray(np.asarray(w_pool, np.float32)[0]),
        "ps_col": _col(np.asarray(pool_scale, np.float32)[0]),
        "w_alpha_aug": np.ascontiguousarray(np.concatenate(
            [np.asarray(w_alpha, np.float32)[0], np.asarray(b_alpha, np.float32)[0][None, :]], axis=0)),
        "gnw_bc": np.ascontiguousarray(np.broadcast_to(np.asarray(gla_norm_w, np.float32)[0][None, :], (128, GLA_DV))),
        "w_out": np.ascontiguousarray(np.asarray(w_out, np.float32)[0]),
        "fnw_bc": np.ascontiguousarray(np.broadcast_to(np.asarray(final_norm_w, np.float32)[None, :], (128, D))),
        "cmat": _const_mats(),
    }
    in_maps = []
    for i in range(N_CORES):
        n_real = SEG * (i + 1)
        n_pad = SEQ - n_real
        xp = np.zeros((SEQ, D), np.float32)
        xp[n_pad:] = x2[:n_real]
        valid = np.zeros((SEQ,), np.float32)
        valid[n_pad:] = 1.0
        vmask = np.ascontiguousarray(valid.reshape(NT, 128).T)
        vh = valid[SEQ - SEG - 16:SEQ - SEG]
        vhalo = np.ascontiguousarray(np.broadcast_to(vh[None, :], (128, 16)))
        tg = SEG * i + np.arange(16)
        inv = np.stack([1.0 / np.minimum(tg + 1, w) for w in POOL_WINDOWS], axis=0).astype(np.float32)
        invcnt = np.ascontiguousarray(np.broadcast_to(inv.reshape(1, 64), (128, 64)))
        m = dict(shared)
        m.update({"x": xp, "vmask": vmask, "vhalo": vhalo, "invcnt": invcnt})
        in_maps.append(m)

    res = run_bass_kernel_spmd(nc, in_maps, core_ids=list(range(N_CORES)))
    outs = [np.asarray(r["out"], dtype=np.float32) for r in res.results]
    return np.concatenate(outs, axis=0).reshape(1, SEQ, D)
```

```python
from contextlib import ExitStack

import numpy as np
import concourse.bass as bass
import concourse.mybir as mybir
from concourse.bass_utils import run_bass_kernel_spmd

F32 = mybir.dt.float32
BF16 = mybir.dt.bfloat16
AF = mybir.ActivationFunctionType
ALU = mybir.AluOpType
AX = mybir.AxisListType

N_CORES = 8
D = 4096
KT = D // 128
D_POOL = 2048
D_GLA = 2048
GLA_HEADS = 4
GLA_DK = 256
GLA_DV = 512
GLA_KEY = 1024
GLA_RANK = 16
GLA_TAU = 16.0
D_IN = 10256
EPS = 1e-6
POOL_WINDOWS = (2, 4, 8, 16)
OFF_U, OFF_GP, OFF_Q, OFF_K, OFF_V, OFF_G, OFF_A = 0, 2048, 4096, 5120, 6144, 8192, 10240
WB = 256

ENGS = ("pe", "act", "dve", "pool", "sp")


class Buf:
    __slots__ = ("name", "last_w", "rd_eng", "rd_dma")

    def __init__(self, name):
        self.name = name
        self.last_w = None
        self.rd_eng = {}
        self.rd_dma = []


class _Op:
    __slots__ = ("id", "eng", "fn", "deps", "marked", "dma", "sem", "val")


class Prog:
    def __init__(self, nc, stack, n_sp=44, n_pool=20):
        self.nc = nc
        self.ops = []
        self.eng_ops = {e: [] for e in ENGS}
        self.last_real = {e: None for e in ENGS}
        self.barrier_deps = {e: set() for e in ENGS}
        self.unconsumed_dma = set()
        self.csem = {e: stack.enter_context(nc.semaphore("c_" + e)) for e in ("pe", "act", "dve", "pool")}
        self.dsem = {
            "sp": [stack.enter_context(nc.semaphore("dsp%d" % i)) for i in range(n_sp)],
            "pool": [stack.enter_context(nc.semaphore("dpl%d" % i)) for i in range(n_pool)],
        }
        self.dsem_rr = {"sp": 0, "pool": 0}
        self.dsem_last = {}

    def op(self, eng, fn, reads=(), writes=(), dma=False):
        o = _Op()
        o.id = len(self.ops)
        o.eng = eng
        o.fn = fn
        o.dma = dma
        o.marked = False
        o.sem = None
        o.val = 0
        deps = set()
        for b in reads:
            if b.last_w is not None:
                deps.add(b.last_w)
        soft = set()
        for b in writes:
            if b.last_w is not None:
                soft.add(b.last_w)
            soft.update(b.rd_eng.values())
            soft.update(b.rd_dma)
        for d in soft:
            od = self.ops[d]
            if (not dma) and (not od.dma) and od.eng == eng:
                continue
            deps.add(d)
        deps.update(self.barrier_deps[eng])
        self.barrier_deps[eng] = set()
        if dma:
            idx = self.dsem_rr[eng]
            self.dsem_rr[eng] = (idx + 1) % len(self.dsem[eng])
            prev = self.dsem_last.get((eng, idx))
            uses = 0
            if prev is not None:
                deps.add(prev[0])
                uses = prev[1]
            o.sem = self.dsem[eng][idx]
            o.val = 16 * (uses + 1)
            self.dsem_last[(eng, idx)] = (o.id, uses + 1)
            self.unconsumed_dma.add(o.id)
        best = {}
        final = []
        for d in deps:
            od = self.ops[d]
            if od.dma:
                final.append(d)
            else:
                if od.eng == "pe" and eng == "pe" and not dma:
                    continue
                if od.eng not in best or best[od.eng] < d:
                    best[od.eng] = d
        final.extend(best.values())
        for d in final:
            od = self.ops[d]
            if od.dma:
                self.unconsumed_dma.discard(d)
            else:
                od.marked = True
        o.deps = sorted(final)
        for b in writes:
            b.last_w = o.id
            b.rd_eng = {}
            b.rd_dma = []
        for b in reads:
            if b.last_w != o.id:
                if dma:
                    b.rd_dma.append(o.id)
                else:
                    b.rd_eng[eng] = o.id
        self.ops.append(o)
        self.eng_ops[eng].append(o)
        if not dma:
            self.last_real[eng] = o.id
        return o.id

    def barrier(self):
        deps = set(self.unconsumed_dma)
        for e in ENGS:
            if e != "sp" and self.last_real[e] is not None:
                deps.add(self.last_real[e])
        for e in ENGS:
            self.barrier_deps[e] |= deps

    def finalize(self):
        for e in ENGS:
            cnt = 0
            for o in self.eng_ops[e]:
                if o.dma:
                    continue
                if o.marked:
                    assert e != "sp"
                    cnt += 1
                    o.sem = self.csem[e]
                    o.val = cnt

    def replay(self, e, eng):
        waited = {}
        for o in self.eng_ops[e]:
            need = {}
            for d in o.deps:
                od = self.ops[d]
                key = od.sem.num
                if waited.get(key, 0) < od.val and need.get(key, (None, 0))[1] < od.val:
                    need[key] = (od.sem, od.val)
            need = list(need.values())
            fuse = need.pop() if (need and e != "sp") else None
            for sem, val in need:
                eng.wait_ge(sem, val)
                waited[sem.num] = val
            ins = o.fn(eng)
            if fuse is not None:
                ins.wait_op(fuse[0], fuse[1], "sem-ge")
                waited[fuse[0].num] = fuse[1]
            if o.dma:
                ins.then_inc(o.sem, 16)
            elif o.marked:
                ins.then_inc(o.sem, 1)

    def run(self):
        self.finalize()
        with self.nc.Block() as block:
            @block.tensor
            def _(eng):
                self.replay("pe", eng)

            @block.scalar
            def _(eng):
                self.replay("act", eng)

            @block.vector
            def _(eng):
                self.replay("dve", eng)

            @block.gpsimd
            def _(eng):
                self.replay("pool", eng)

            @block.sync
            def _(eng):
                self.replay("sp", eng)


class _Rot:
    def __init__(self, items):
        self.items = items
        self.i = 0

    def next(self):
        it = self.items[self.i]
        self.i = (self.i + 1) % len(self.items)
        return it


def build_program(SEQ):
    SEG = SEQ // N_CORES
    NB = N_CORES
    NTB = SEG // 128
    NT = SEQ // 128
    TW = min(256, SEG)
    NHF = SEG // TW
    TPW = TW // 128
    L = SEG + 16

    nc = bass.Bass("TRN2", target_bir_lowering=False)

    def din(name, shape):
        return nc.dram_tensor(name, list(shape), F32, kind="ExternalInput").ap()

    x_d = din("x", [SEQ, D])
    vmask_d = din("vmask", [128, NT])
    vhalo_d = din("vhalo", [128, 16])
    invcnt_d = din("invcnt", [128, 4 * 16])
    ccol_d = din("c_col", [128, KT])
    wada_d = din("w_ada", [D, 3 * D])
    bada_d = din("b_ada_col", [128, 96])
    nw_d = din("nw_col", [128, KT])
    win_d = din("w_in", [D, D_IN])
    wpool_d = din("w_pool", [4, 512, 512])
    pscol_d = din("ps_col", [128, 16])
    walpha_d = din("w_alpha_aug", [17, GLA_KEY])
    gnw_d = din("gnw_bc", [128, GLA_DV])
    wout_d = din("w_out", [D, D])
    fnw_d = din("fnw_bc", [128, D])
    cmat_d = din("cmat", [128, 3 * 128])
    out_d = nc.dram_tensor("out", [SEG, D], F32, kind="ExternalOutput").ap()
    ymd = nc.dram_tensor("ymixT_d", [32, 128, SEG], BF16, kind="Internal").ap()
    ymd_b = [Buf("ymd%d" % i) for i in range(8)]
    out_b = [Buf("out%d" % i) for i in range(NTB)]

    with ExitStack() as st:
        P = Prog(nc, st)

        def sb(name, shape, dt):
            return st.enter_context(nc.sbuf_tensor("sb_" + name, list(shape), dt))

        hT = sb("hT", [128, KT, SEG], BF16)
        hT_tb = [Buf("hT%d" % t) for t in range(NTB)]

        def hTl(hf):
            return hT_tb[hf * TPW:(hf + 1) * TPW]
        Wt = [sb("W%d" % i, [128, KT, WB], BF16) for i in range(3)]
        W_rot = _Rot([(Wt[i], Buf("W%d" % i)) for i in range(3)])
        S = sb("S", [128, GLA_HEADS, 2, GLA_DV], F32)
        S_b = [[Buf("S%d%d" % (h, k)) for k in range(2)] for h in range(GLA_HEADS)]
        PH_WORDS = 15488
        PH = sb("PH", [128, PH_WORDS], F32)
        cm = sb("cmat", [128, 3 * 128], F32)
        cm_b = Buf("cmat")
        ident = cm[:, 0:128]
        tri = cm[:, 128:256]
        ustr = cm[:, 256:384]
        ident_bf = sb("ident_bf", [128, 128], BF16)
        identbf_b = Buf("identbf")
        ones_mat = sb("ones_mat", [128, 128], F32)
        ones_b = Buf("ones")
        eps_t = sb("eps_t", [128, 1], F32)
        eps_b = Buf("eps")
        vmask = sb("vmask", [128, NT], F32)
        vmask_b = Buf("vmask")
        vhalo = sb("vhalo", [128, 16], F32)
        vhalo_b = Buf("vhalo")
        invcnt = sb("invcnt", [128, 64], F32)
        invcnt_b = Buf("invcnt")
        ccol = sb("ccol", [128, KT], F32)
        ccol_b = Buf("ccol")
        cact = sb("cact", [128, KT], F32)
        cact_b = Buf("cact")
        cbf = sb("cbf", [128, KT], BF16)
        cbf_b = Buf("cbf")
        bada = sb("bada", [128, 96], F32)
        bada_b = Buf("bada")
        nwc = sb("nwc", [128, KT], F32)
        nwc_b = Buf("nwc")
        modc = sb("modc", [128, 96], F32)
        modc_b = Buf("modc")
        scol = sb("scol", [128, KT], F32)
        scol_b = Buf("scol")
        pscol = sb("pscol", [128, 16], F32)
        pscol_b = Buf("pscol")
        walpha_t = [sb("walpha%d" % i, [17, GLA_DK], F32) for i in range(2)]
        walpha_rot = _Rot([(walpha_t[i], Buf("walpha%d" % i)) for i in range(2)])
        walpha_pre = {}
        gnw = sb("gnw", [128, GLA_DV], F32)
        gnw_b = Buf("gnw")
        alrT = sb("alrT", [32, SEG], F32)
        alrT_b = Buf("alrT")
        Walr = sb("Walr", [128, KT, GLA_RANK], BF16)
        Walr_b = Buf("Walr")
        eb = sb("eb", [128, NTB, 8], F32)
        eb_b = [[Buf("eb%d_%d" % (t, h)) for h in range(GLA_HEADS)] for t in range(NTB)]
        hTh = sb("hTh", [128, KT, 16], BF16)
        hTh_b = Buf("hTh")
        smalls = sb("smalls", [128, 64], F32)
        small_rot = _Rot([(smalls[:, i:i + 1], Buf("sm%d" % i)) for i in range(64)])
        ssq = sb("ssq", [128, NTB * 16], F32)
        ssq_b = Buf("ssq")
        fin = sb("fin", [128, 3 * NTB], F32)
        fin_b = Buf("fin")

        pall = [st.enter_context(nc.psum_tensor("pall%d" % i, [128, 512], F32)) for i in range(7)]
        pall_rot = _Rot([(pall[i], Buf("pall%d" % i)) for i in range(7)])
        pacc_rot = pall_rot
        paux_rot = pall_rot

        class Carve:
            def __init__(self):
                self.off = 0

            def f32(self, n, shape=None):
                a = PH[:, self.off:self.off + n]
                self.off += n
                assert self.off <= PH_WORDS
                if shape is not None:
                    a = a.rearrange("p (a b) -> p a b", b=shape[-1]) if len(shape) == 2 else a
                return a

            def bf16(self, n, shape=None):
                words = (n + 1) // 2
                a = PH[:, self.off:self.off + words].bitcast(BF16)
                self.off += words
                assert self.off <= PH_WORDS
                if shape is not None and len(shape) == 2:
                    a = a.rearrange("p (a b) -> p a b", b=shape[-1])
                return a

        def dma(q, out, in_, reads=(), writes=()):
            return P.op(q, lambda e: e.dma_start(out=out, in_=in_), reads=reads, writes=writes, dma=True)

        def mm(out, lhsT, rhs, start, stop, reads, writes):
            return P.op("pe", lambda e: e.matmul(out, lhsT=lhsT, rhs=rhs, start=start, stop=stop),
                        reads=reads, writes=writes)

        def act(out, in_, func, reads, writes, bias=None, scale=None, accum_out=None):
            kw = {}
            if bias is not None:
                kw["bias"] = bias
            if scale is not None:
                kw["scale"] = scale
            if accum_out is not None:
                kw["accum_out"] = accum_out
            return P.op("act", lambda e: e.activation(out=out, in_=in_, func=func, **kw),
                        reads=reads, writes=writes)

        def tsc(out, in0, s1, s2, op0, op1, reads, writes):
            if op1 is None:
                return P.op("dve", lambda e: e.tensor_scalar(out=out, in0=in0, scalar1=s1, scalar2=None, op0=op0),
                            reads=reads, writes=writes)
            return P.op("dve", lambda e: e.tensor_scalar(out=out, in0=in0, scalar1=s1, scalar2=s2, op0=op0, op1=op1),
                        reads=reads, writes=writes)

        def stt(out, in0, scalar, in1, op0, op1, reads, writes):
            return P.op("dve", lambda e: e.scalar_tensor_tensor(out=out, in0=in0, scalar=scalar, in1=in1,
                                                                 op0=op0, op1=op1), reads=reads, writes=writes)

        def tt(out, in0, in1, op, reads, writes):
            return P.op("dve", lambda e: e.tensor_tensor(out=out, in0=in0, in1=in1, op=op),
                        reads=reads, writes=writes)

        def vcopy(out, in_, reads, writes):
            return P.op("dve", lambda e: e.tensor_copy(out=out, in_=in_), reads=reads, writes=writes)

        def recip(out, in_, reads, writes):
            return P.op("dve", lambda e: e.reciprocal(out=out, in_=in_), reads=reads, writes=writes)

        def memset(ap, val, writes):
            return P.op("dve", lambda e: e.memset(ap, val), writes=writes)

        def load_w(src2d, c0, width):
            wt, wb = W_rot.next()
            src = src2d.rearrange("(kt p) c -> p kt c", p=128)[:, :, c0:c0 + width]
            dma("pool", wt[:, :, 0:width], src, writes=[wb])
            return wt, wb

        dma("sp", cm[:], cmat_d[:, :], writes=[cm_b])
        dma("sp", vmask[:], vmask_d[:, :], writes=[vmask_b])
        dma("sp", vhalo[:], vhalo_d[:, :], writes=[vhalo_b])
        dma("sp", invcnt[:], invcnt_d[:, :], writes=[invcnt_b])
        dma("sp", ccol[:], ccol_d[:, :], writes=[ccol_b])
        dma("sp", bada[:], bada_d[:, :], writes=[bada_b])
        dma("sp", nwc[:], nw_d[:, :], writes=[nwc_b])
        dma("sp", pscol[:], pscol_d[:, :], writes=[pscol_b])
        dma("sp", gnw[:], gnw_d[:, :], writes=[gnw_b])
        dma("pool", Walr[:], win_d.rearrange("(kt p) c -> p kt c", p=128)[:, :, OFF_A:OFF_A + GLA_RANK],
            writes=[Walr_b])
        memset(ones_mat[:], 1.0, [ones_b])
        memset(eps_t[:], EPS, [eps_b])
        memset(alrT[:], 1.0, [alrT_b])
        memset(S[:], 0.0, [b for hb in S_b for b in hb])
        vcopy(ident_bf[:], ident, [cm_b], [identbf_b])
        ones_col = ones_mat[:, 0:1]

        act(cact[:], ccol[:], AF.Silu, [ccol_b], [cact_b])
        vcopy(cbf[:], cact[:], [cact_b], [cbf_b])
        pm = st.enter_context(nc.psum_tensor("pmod", [128, 512], F32))
        pm_b = Buf("pmod")
        modg_b = Buf("modg")

        def run_tasks(items):
            w_idx = [i for i, it in enumerate(items) if it[0] == "w"]
            loaded = {}
            state = {"nxt": 0}

            def ensure(upto):
                while state["nxt"] < len(w_idx) and state["nxt"] <= upto:
                    it = items[w_idx[state["nxt"]]]
                    loaded[w_idx[state["nxt"]]] = load_w(it[1], it[2], WB)
                    state["nxt"] += 1

            wpos = 0
            ensure(1)
            for i, it in enumerate(items):
                if it[0] == "f":
                    it[1]()
                else:
                    wt, wb = loaded.pop(i)
                    it[3](wt, wb)
                    wpos += 1
                    ensure(wpos + 1)

        def ada_block(blk):
            def f(wt, wb):
                for c in range(WB // 128):
                    ctg = blk * (WB // 128) + c
                    for kt in range(KT):
                        mm(pm[:, ctg:ctg + 1], wt[:, kt, c * 128:(c + 1) * 128], cbf[:, kt:kt + 1],
                           kt == 0, kt == KT - 1, [wb, cbf_b], [pm_b])
            return ("w", wada_d, blk * WB, f)

        shiftc = modc[:, 0:KT]
        gatec = modc[:, 2 * KT:3 * KT]

        def mod_finish_ss():
            tt(modc[:, 0:2 * KT], pm[:, 0:2 * KT], bada[:, 0:2 * KT], ALU.add, [pm_b, bada_b], [modc_b])
            tsc(scol[:], modc[:, KT:2 * KT], 1.0, None, ALU.add, None, [modc_b], [scol_b])
            tt(scol[:], scol[:], nwc[:], ALU.mult, [scol_b, nwc_b], [scol_b])

        def mod_finish_gate():
            tt(gatec, pm[:, 2 * KT:3 * KT], bada[:, 2 * KT:3 * KT], ALU.add, [pm_b, bada_b], [modg_b])

        n_ss_blk = (2 * D) // WB
        n_ada_blk = (3 * D) // WB
        items = [ada_block(blk) for blk in range(n_ss_blk)]
        items.append(("f", mod_finish_ss))
        gate_blks = list(range(n_ss_blk, n_ada_blk))

        cv = Carve()
        xst = [(cv.f32(4096), Buf("xst0")), (cv.f32(4096), Buf("xst1"))]
        junk = cv.bf16(4096)
        junk_b = Buf("junk")

        def x_load(tt_):
            xt, xb = xst[tt_ % 2]
            dma("sp", xt, x_d[tt_ * 128:(tt_ + 1) * 128, :], writes=[xb])

        dgs = [(cv.f32(128), Buf("dg0")), (cv.f32(128), Buf("dg1"))]
        ph_mark = cv.off

        def h_stats(tt_):
            xt, xb = xst[tt_ % 2]
            dgt, dg_b = dgs[tt_ % 2]
            ss, ss_b = small_rot.next()
            rt, rt_b = small_rot.next()
            rs, rs_b = small_rot.next()
            act(junk, xt, AF.Square, [xb], [junk_b, ss_b], accum_out=ss)
            act(rt, ss, AF.Sqrt, [ss_b, eps_b], [rt_b], bias=eps_t[:], scale=1.0 / D)
            recip(rs, rt, [rt_b], [rs_b])
            tsc(dgt, ident, rs, None, ALU.mult, None, [cm_b, rs_b], [dg_b])

        def h_trans(tt_, tloc):
            xt, xb = xst[tt_ % 2]
            dgt, dg_b = dgs[tt_ % 2]
            for g4 in range(KT // 4):
                pt, pt_b = paux_rot.next()
                for j in range(4):
                    kt = g4 * 4 + j
                    mm(pt[:, j * 128:(j + 1) * 128], xt[:, kt * 128:(kt + 1) * 128], dgt, True, True,
                       [xb, dg_b], [pt_b])
                for j in range(4):
                    kt = g4 * 4 + j
                    if g4 % 4 == 1:
                        act(hT[:, kt, tloc * 128:(tloc + 1) * 128], pt[:, j * 128:(j + 1) * 128], AF.Identity,
                            [pt_b, scol_b, modc_b], [hT_tb[tloc]], bias=shiftc[:, kt:kt + 1],
                            scale=scol[:, kt:kt + 1])
                    else:
                        tsc(hT[:, kt, tloc * 128:(tloc + 1) * 128], pt[:, j * 128:(j + 1) * 128],
                            scol[:, kt:kt + 1], shiftc[:, kt:kt + 1], ALU.mult, ALU.add,
                            [pt_b, scol_b, modc_b], [hT_tb[tloc]])

        def alr_block():
            for hf in range(NHF):
                pa, pa_b = paux_rot.next()
                for kt in range(KT):
                    mm(pa[0:GLA_RANK, 0:TW], Walr[:, kt, :], hT[:, kt, hf * TW:(hf + 1) * TW],
                       kt == 0, kt == KT - 1, [Walr_b] + hTl(hf), [pa_b])
                act(alrT[0:GLA_RANK, hf * TW:(hf + 1) * TW], pa[0:GLA_RANK, 0:TW], AF.Copy, [pa_b], [alrT_b])

        def k_tm_pass(hd, blk_base_tile, wt, wb, kd, kd_b, tmp, la_keep=None, pre_tile=None, own_mode=False):
            pend_t = None
            if hd not in walpha_pre:
                wa_ = walpha_rot.next()
                dma("sp", wa_[0][:], walpha_d[:, hd * GLA_DK:(hd + 1) * GLA_DK], writes=[wa_[1]])
                walpha_pre[hd] = wa_
            walpha, walpha_b = walpha_pre.pop(hd)
            nh = (hd + 1) % GLA_HEADS
            wa_ = walpha_rot.next()
            dma("sp", wa_[0][:], walpha_d[:, nh * GLA_DK:(nh + 1) * GLA_DK], writes=[wa_[1]])
            walpha_pre[nh] = wa_

            def second_half(t, la_sp, la_sp_b):
                pu, pu_b = paux_rot.next()
                if own_mode:
                    for k2 in range(2):
                        mm(pu[:, 256 + k2:257 + k2], la_sp[:, k2 * 128:(k2 + 1) * 128], ones_col, True, True,
                           [la_sp_b, ones_b], [pu_b])
                    act(eb[:, t, hd * 2:hd * 2 + 2], pu[:, 256:258], AF.Exp, [pu_b], [eb_b[t][hd]],
                        scale=-1.0 / GLA_TAU)
                    return
                mm(pu[:, 0:256], ustr, la_sp, True, True, [cm_b, la_sp_b], [pu_b])
                for k2 in range(2):
                    mm(pu[:, 256 + k2:257 + k2], la_sp[:, k2 * 128:(k2 + 1) * 128], ones_col, True, True,
                       [la_sp_b, ones_b], [pu_b])
                dk, dk_b = tmp["dk"]
                act(dk, pu[:, 0:256], AF.Exp, [pu_b], [dk_b], scale=-1.0 / GLA_TAU)
                act(eb[:, t, hd * 2:hd * 2 + 2], pu[:, 256:258], AF.Exp, [pu_b], [eb_b[t][hd]],
                    scale=-1.0 / GLA_TAU)
                pk, pk_b = tmp["pk"][t]
                stt(kd[:, t, :], pk[:, 0:256], vmask[:, blk_base_tile + t:blk_base_tile + t + 1], dk,
                    ALU.mult, ALU.mult, [pk_b, vmask_b, dk_b], [kd_b])

            tmp["pk"] = {}
            for t in range(NTB):
                if pre_tile is not None:
                    pre_tile(t)
                if not own_mode:
                    pk, pk_b = pacc_rot.next()
                    tmp["pk"][t] = (pk, pk_b)
                    for kt in range(KT):
                        mm(pk[:, 0:256], hT[:, kt, t * 128:(t + 1) * 128], wt[:, kt, 0:256],
                           kt == 0, kt == KT - 1, [hT_tb[t], wb], [pk_b])
                pl, pl_b = paux_rot.next()
                mm(pl[:, 0:256], alrT[0:17, t * 128:(t + 1) * 128], walpha[0:17, 0:GLA_DK],
                   True, True, [alrT_b, walpha_b], [pl_b])
                la_e, la_e_b = tmp["la_e"]
                if la_keep is not None:
                    la_sp, la_sp_b = la_keep[0][:, t, :], la_keep[1][t]
                else:
                    la_sp, la_sp_b = tmp["la_sp"][t % 2]
                act(la_e, pl[:, 0:256], AF.Exp, [pl_b], [la_e_b], scale=-1.0)
                act(la_sp, la_e, AF.Ln, [la_e_b, ones_b], [la_sp_b], bias=ones_col)
                if pend_t is not None:
                    second_half(*pend_t)
                pend_t = (t, la_sp, la_sp_b)
            second_half(*pend_t)

        def v_tm_pass(half, wt, wb, v, v_b):
            for t in range(NTB):
                pv, pv_b = pacc_rot.next()
                for kt in range(KT):
                    mm(pv[:, 0:256], hT[:, kt, t * 128:(t + 1) * 128], wt[:, kt, 0:256],
                       kt == 0, kt == KT - 1, [hT_tb[t], wb], [pv_b])
                act(v[:, t, half * 256:(half + 1) * 256], pv[:, 0:256], AF.Copy, [pv_b], [v_b[t]])

        def state_update(hd, t, kd, kd_b, v, v_b, sbf=None):
            for k2 in range(2):
                ps_, ps_b = pacc_rot.next()
                mm(ps_[:, :], kd[:, t, k2 * 128:(k2 + 1) * 128], v[:, t, :], True, True,
                   [kd_b, v_b[t]], [ps_b])
                stt(S[:, hd, k2, :], S[:, hd, k2, :], eb[:, t, hd * 2 + k2:hd * 2 + k2 + 1], ps_[:, :],
                    ALU.mult, ALU.add, [S_b[hd][k2], eb_b[t][hd], ps_b], [S_b[hd][k2]])
                if sbf is not None:
                    act(sbf[0][:, hd, k2, :], S[:, hd, k2, :], AF.Copy, [S_b[hd][k2]], [sbf[1][hd][k2]])

        kd_p = cv.bf16(NTB * 256, (NTB, 256))
        kd_p_b = Buf("kd_p")
        v_p = cv.bf16(NTB * 512, (NTB, 512))
        v_p_b = [Buf("v_p%d" % t) for t in range(NTB)]
        tmp_p = {
            "la_e": (cv.f32(256), Buf("la_e")),
            "la_sp": [(cv.f32(256), Buf("la_sp0")), (cv.f32(256), Buf("la_sp1"))],
            "dk": (cv.f32(256), Buf("dk")),
        }

        def h_phase0():
            def f():
                x_load(0)
                x_load(1)
                h_stats(0)
                for t in range(NTB):
                    if t + 1 < NTB:
                        h_stats(t + 1)
                    h_trans(t, t)
                    if t + 2 < NT:
                        x_load(t + 2)
                alr_block()
            return ("f", f)

        def k_task(b, hd):
            def f(wt, wb):
                k_tm_pass(hd, b * NTB, wt, wb, kd_p, kd_p_b, tmp_p)
            return ("w", win_d, OFF_K + hd * 256, f)

        def v_task(b, hd, half):
            def f(wt, wb):
                if half == 0:
                    v_tm_pass(0, wt, wb, v_p, v_p_b)
                    return
                for t in range(NTB):
                    pv, pv_b = pacc_rot.next()
                    for kt in range(KT):
                        mm(pv[:, 0:256], hT[:, kt, t * 128:(t + 1) * 128], wt[:, kt, 0:256],
                           kt == 0, kt == KT - 1, [hT_tb[t], wb], [pv_b])
                    act(v_p[:, t, 256:512], pv[:, 0:256], AF.Copy, [pv_b], [v_p_b[t]])
                    if t > 0:
                        state_update(hd, t - 1, kd_p, kd_p_b, v_p, v_p_b)
                state_update(hd, NTB - 1, kd_p, kd_p_b, v_p, v_p_b)
            return ("w", win_d, OFF_V + hd * 512 + half * 256, f)

        held = []

        def v_hold(b, hd):
            def f(wt, wb):
                held.append((wt, wb))
            return ("w", win_d, OFF_V + hd * 512, f)

        def v_last(b, hd):
            def f(wt2, wb2):
                wt1, wb1 = held.pop()
                if b + 1 == NB - 1:
                    vcopy(hTh[:], hT[:, :, SEG - 16:SEG], [hT_tb[NTB - 1]], [hTh_b])
                h_stats((b + 1) * NTB)
                for t in range(NTB):
                    for half, (wt, wb) in enumerate(((wt1, wb1), (wt2, wb2))):
                        pv, pv_b = pacc_rot.next()
                        for kt in range(KT):
                            mm(pv[:, 0:256], hT[:, kt, t * 128:(t + 1) * 128], wt[:, kt, 0:256],
                               kt == 0, kt == KT - 1, [hT_tb[t], wb], [pv_b])
                        act(v_p[:, t, half * 256:(half + 1) * 256], pv[:, 0:256], AF.Copy, [pv_b], [v_p_b[t]])
                    tt_ = (b + 1) * NTB + t
                    if t + 1 < NTB:
                        h_stats(tt_ + 1)
                    if t > 0:
                        state_update(hd, t - 1, kd_p, kd_p_b, v_p, v_p_b)
                    h_trans(tt_, t)
                    if tt_ + 2 < NT:
                        x_load(tt_ + 2)
                state_update(hd, NTB - 1, kd_p, kd_p_b, v_p, v_p_b)
                if b + 1 < NB - 1:
                    alr_block()
            return ("w", win_d, OFF_V + hd * 512 + 256, f)

        items.append(h_phase0())
        for b in range(NB - 1):
            for hd in range(GLA_HEADS):
                items.append(k_task(b, hd))
                if hd < GLA_HEADS - 1:
                    items.append(v_task(b, hd, 0))
                    items.append(v_task(b, hd, 1))
                else:
                    items.append(v_hold(b, hd))
                    items.append(v_last(b, hd))
                if gate_blks:
                    items.append(ada_block(gate_blks.pop(0)))
                    if not gate_blks:
                        items.append(("f", mod_finish_gate))
        assert not gate_blks
        run_tasks(items)

        pre_gla = [load_w(win_d, OFF_K, 256), load_w(win_d, OFF_Q, 256)]
        P.barrier()
        cv = Carve()
        kd_o = cv.bf16(NTB * 256, (NTB, 256))
        kd_o_b = Buf("kd_o")
        v_o = cv.bf16(NTB * 512, (NTB, 512))
        v_o_b = [Buf("v_o%d" % t) for t in range(NTB)]
        keT = cv.bf16(2 * SEG, (2, SEG))
        keT_b = Buf("keT")
        qeT = cv.bf16(2 * SEG, (2, SEG))
        qeT_b = Buf("qeT")
        gp = cv.f32(NTB * 512, (NTB, 512))
        gp_b = [Buf("gp%d" % t) for t in range(NTB)]
        lasp = cv.f32(NTB * 256, (NTB, 256))
        lasp_b = [Buf("lasp%d" % t) for t in range(NTB)]
        Sbf = cv.bf16(GLA_HEADS * 2 * 512).rearrange("p (h k e) -> p h k e", h=GLA_HEADS, k=2)
        Sbf_b = [[Buf("Sbf%d%d" % (h, k)) for k in range(2)] for h in range(GLA_HEADS)]
        tmp_o = {"la_e": (cv.f32(256), Buf("la_e_o"))}
        kdTs = [(cv.bf16(TW), Buf("kdT0")), (cv.bf16(TW), Buf("kdT1"))]
        kd_pending = []
        kd_i = [0]
        eqt = (cv.f32(512), Buf("eqt"))
        ekt = (cv.f32(512), Buf("ekt"))
        gs = (eqt[0][:, 0:256], eqt[1])
        junk_o = (ekt[0].bitcast(BF16)[:, 0:512], ekt[1])
        ATbf = (cv.bf16(128), Buf("ATbf"))
        y_t = (cv.bf16(512), Buf("y_t"))
        yT_sb = (cv.bf16(512, (4, 128)), Buf("yT_sb"))

        alr_block()
        for hd in range(GLA_HEADS):
            for k2 in range(2):
                act(Sbf[:, hd, k2, :], S[:, hd, k2, :], AF.Copy, [S_b[hd][k2]], [Sbf_b[hd][k2]])

        own_tile0 = (NB - 1) * NTB
        for hd in range(GLA_HEADS):
            if hd == 0:
                wk, wq = pre_gla
            else:
                wk = load_w(win_d, OFF_K + hd * 256, 256)
                wq = load_w(win_d, OFF_Q + hd * 256, 256)
            k_tm_pass(hd, own_tile0, wk[0], wk[1], kd_o, kd_o_b, tmp_o, la_keep=(lasp, lasp_b), own_mode=True)
            wv0 = load_w(win_d, OFF_V + hd * 512, 256)
            for k2 in range(2):
                for hf in range(NHF):
                    pbT, pbT_b = paux_rot.next()
                    for j in range(TPW):
                        t = hf * TPW + j
                        mm(pbT[:, j * 128:(j + 1) * 128], lasp[:, t, k2 * 128:(k2 + 1) * 128], tri, True, True,
                           [lasp_b[t], cm_b], [pbT_b])
                    act(eqt[0][:, 0:TW], pbT[:, 0:TW], AF.Exp, [pbT_b], [eqt[1]], scale=-1.0 / GLA_TAU)
                    act(ekt[0][:, 0:TW], pbT[:, 0:TW], AF.Exp, [pbT_b], [ekt[1]], scale=1.0 / GLA_TAU)
                    pq, pq_b = pacc_rot.next()
                    for kt in range(KT):
                        mm(pq[:, 0:TW], wq[0][:, kt, k2 * 128:(k2 + 1) * 128], hT[:, kt, hf * TW:(hf + 1) * TW],
                           kt == 0, kt == KT - 1, [wq[1]] + hTl(hf), [pq_b])
                    pk2, pk2_b = pacc_rot.next()
                    for kt in range(KT):
                        mm(pk2[:, 0:TW], wk[0][:, kt, k2 * 128:(k2 + 1) * 128], hT[:, kt, hf * TW:(hf + 1) * TW],
                           kt == 0, kt == KT - 1, [wk[1]] + hTl(hf), [pk2_b])
                    while kd_pending:
                        kd_pending.pop(0)()
                    stt(qeT[:, k2, hf * TW:(hf + 1) * TW], pq[:, 0:TW], GLA_DK ** -0.5, eqt[0][:, 0:TW],
                        ALU.mult, ALU.mult, [pq_b, eqt[1]], [qeT_b])
                    tt(keT[:, k2, hf * TW:(hf + 1) * TW], pk2[:, 0:TW], ekt[0][:, 0:TW], ALU.mult,
                       [pk2_b, ekt[1]], [keT_b])
                    kdT_t, kdT_b = kdTs[kd_i[0] % 2]
                    kd_i[0] += 1
                    for j in range(TPW):
                        t = hf * TPW + j
                        stt(kdT_t[:, j * 128:(j + 1) * 128], pk2[:, j * 128:(j + 1) * 128],
                            eb[:, t, hd * 2 + k2:hd * 2 + k2 + 1], ekt[0][:, j * 128:(j + 1) * 128],
                            ALU.mult, ALU.mult, [pk2_b, eb_b[t][hd], ekt[1]], [kdT_b])

                    def flush(k2=k2, hf=hf, kdT_t=kdT_t, kdT_b=kdT_b):
                        pkd_t, pkd_b = pall_rot.next()
                        pkd = pkd_t[:, :].bitcast(BF16)
                        for j in range(TPW):
                            P.op("pe", lambda e, o=pkd[:, j * 128:(j + 1) * 128], i=kdT_t[:, j * 128:(j + 1) * 128]:
                                 e.transpose(out=o, in_=i, identity=ident_bf[:]), reads=[kdT_b, identbf_b],
                                 writes=[pkd_b])
                        for j in range(TPW):
                            t = hf * TPW + j
                            act(kd_o[:, t, k2 * 128:(k2 + 1) * 128], pkd[:, j * 128:(j + 1) * 128], AF.Copy,
                                [pkd_b], [kd_o_b])
                    kd_pending.append(flush)
            while kd_pending:
                kd_pending.pop(0)()
            wv1 = load_w(win_d, OFF_V + hd * 512 + 256, 256)
            v_tm_pass(0, wv0[0], wv0[1], v_o, v_o_b)
            wg0 = load_w(win_d, OFF_G + hd * 512, 256)
            v_tm_pass(1, wv1[0], wv1[1], v_o, v_o_b)
            wg1 = load_w(win_d, OFF_G + hd * 512 + 256, 256)
            def emit_yT(t, hd=hd):
                pbf_t, pbf_b = pall_rot.next()
                pbf = pbf_t[:, :].bitcast(BF16)
                for j in range(4):
                    P.op("pe", lambda e, o=pbf[:, j * 128:(j + 1) * 128], i=y_t[0][:, j * 128:(j + 1) * 128]:
                         e.transpose(out=o, in_=i, identity=ident_bf[:]), reads=[y_t[1], identbf_b],
                         writes=[pbf_b])
                act(yT_sb[0], pbf[:, 0:512].rearrange("p (a b) -> p a b", b=128), AF.Copy, [pbf_b], [yT_sb[1]])
                mt0 = 16 + hd * 4
                dma("sp", ymd[mt0:mt0 + 4, :, t * 128:(t + 1) * 128].rearrange("j p c -> p j c"), yT_sb[0],
                    reads=[yT_sb[1]], writes=[ymd_b[4 + hd]])

            for t in range(NTB):
                pA, pA_b = paux_rot.next()
                for k2 in range(2):
                    mm(pA[:, 0:128], keT[:, k2, t * 128:(t + 1) * 128], qeT[:, k2, t * 128:(t + 1) * 128],
                       k2 == 0, k2 == 1, [keT_b, qeT_b], [pA_b])
                tt(ATbf[0], pA[:, 0:128], tri, ALU.mult, [pA_b, cm_b], [ATbf[1]])
                for half, wg in ((0, wg0), (1, wg1)):
                    pg, pg_b = pacc_rot.next()
                    for kt in range(KT):
                        mm(pg[:, 0:256], hT[:, kt, t * 128:(t + 1) * 128], wg[0][:, kt, 0:256],
                           kt == 0, kt == KT - 1, [hT_tb[t], wg[1]], [pg_b])
                    act(gs[0], pg[:, 0:256], AF.Silu, [pg_b], [gs[1]])
                    tt(gp[:, t, half * 256:(half + 1) * 256], gs[0], gnw[:, half * 256:(half + 1) * 256], ALU.mult,
                       [gs[1], gnw_b], [gp_b[t]])
                if t > 0:
                    emit_yT(t - 1)
                po, po_b = pacc_rot.next()
                for k2 in range(2):
                    mm(po[:, :], qeT[:, k2, t * 128:(t + 1) * 128], Sbf[:, hd, k2, :], k2 == 0, False,
                       [qeT_b, Sbf_b[hd][k2]], [po_b])
                mm(po[:, :], ATbf[0], v_o[:, t, :], False, True, [ATbf[1], v_o_b[t]], [po_b])
                so, so_b = small_rot.next()
                ro, ro_b = small_rot.next()
                rso, rso_b = small_rot.next()
                act(junk_o[0], po[:, :], AF.Square, [po_b], [junk_o[1], so_b], accum_out=so)
                act(ro, so, AF.Sqrt, [so_b, eps_b], [ro_b], bias=eps_t[:], scale=1.0 / GLA_DV)
                recip(rso, ro, [ro_b], [rso_b])
                stt(y_t[0], po[:, :], rso, gp[:, t, :], ALU.mult, ALU.mult, [po_b, rso_b, gp_b[t]], [y_t[1]])
                state_update(hd, t, kd_o, kd_o_b, v_o, v_o_b, sbf=(Sbf, Sbf_b))
            emit_yT(NTB - 1)

        pre_pool = [load_w(win_d, OFF_U, 256), load_w(win_d, OFF_U + 256, 256)]
        P.barrier()
        cv = Carve()
        u_t = (cv.f32(L), Buf("u_t"))
        pa_t = (cv.f32(L), Buf("pa_t"))
        pb_t = (cv.f32(L), Buf("pb_t"))
        t16 = (cv.f32(16), Buf("t16"))
        pooled = cv.bf16(4 * SEG, (4, SEG))
        pooled_b = [Buf("pooled%d" % c) for c in range(4)]
        gps_all = cv.f32(4 * SEG, (4, SEG))
        gps_all_b = [Buf("gps%d" % i) for i in range(4)]
        ypT = cv.bf16(4 * SEG, (4, SEG))
        ypT_b = Buf("ypT")
        wp_all = cv.bf16(16 * 512).rearrange("p (g c d) -> p g c d", g=4, c=4)
        wp_b = Buf("wp")
        dma("pool", wp_all, wpool_d.rearrange("g (ct p) d -> p g ct d", p=128), writes=[wp_b])

        for g in range(4):
            w = POOL_WINDOWS[g]
            wp = wp_all[:, g, :, :]
            for blk in range(2):
                if g == 0:
                    wu = pre_pool[blk]
                else:
                    wu = load_w(win_d, OFF_U + g * 512 + blk * 256, 256)
                for c in range(2):
                    ct = blk * 2 + c
                    for hf in range(NHF):
                        pu, pu_b = pacc_rot.next()
                        for kt in range(KT):
                            mm(pu[:, 0:TW], wu[0][:, kt, c * 128:(c + 1) * 128], hT[:, kt, hf * TW:(hf + 1) * TW],
                               kt == 0, kt == KT - 1, [wu[1]] + hTl(hf), [pu_b])
                        act(u_t[0][:, 16 + hf * TW:16 + (hf + 1) * TW], pu[:, 0:TW], AF.Copy, [pu_b], [u_t[1]])
                    ph_, ph_b = paux_rot.next()
                    for kt in range(KT):
                        mm(ph_[:, 0:16], wu[0][:, kt, c * 128:(c + 1) * 128], hTh[:, kt, :],
                           kt == 0, kt == KT - 1, [wu[1], hTh_b], [ph_b])
                    tt(u_t[0][:, 0:16], ph_[:, 0:16], vhalo[:], ALU.mult, [ph_b, vhalo_b], [u_t[1]])
                    src, dst = u_t, pa_t
                    sh = 1
                    other = pb_t
                    for step in range(g + 1):
                        lo = 2 * sh - 1
                        tt(dst[0][:, lo:L], src[0][:, lo:L], src[0][:, lo - sh:L - sh], ALU.add,
                           [src[1]], [dst[1]])
                        src = dst
                        dst, other = other, dst
                        sh *= 2
                    win = src
                    stt(pooled[:, ct, :], win[0][:, 16:L], 1.0 / w, u_t[0][:, 16:L], ALU.mult, ALU.subtract,
                        [win[1], u_t[1]], [pooled_b[ct]])
                    tt(t16[0], win[0][:, 16:32], invcnt[:, g * 16:(g + 1) * 16], ALU.mult,
                       [win[1], invcnt_b], [t16[1]])
                    tt(pooled[:, ct, 0:16], t16[0], u_t[0][:, 16:32], ALU.subtract, [t16[1], u_t[1]],
                       [pooled_b[ct]])
            for blk in range(2):
                wg_ = load_w(win_d, OFF_GP + g * 512 + blk * 256, 256)
                for c in range(2):
                    dt_ = blk * 2 + c
                    for hf in range(NHF):
                        pg, pg_b = pacc_rot.next()
                        for kt in range(KT):
                            mm(pg[:, 0:TW], wg_[0][:, kt, c * 128:(c + 1) * 128], hT[:, kt, hf * TW:(hf + 1) * TW],
                               kt == 0, kt == KT - 1, [wg_[1]] + hTl(hf), [pg_b])
                        act(gps_all[:, dt_, hf * TW:(hf + 1) * TW], pg[:, 0:TW], AF.Silu, [pg_b], [gps_all_b[dt_]])
            for dt_ in range(4):
                for hf in range(NHF):
                    pm_, pm_b2 = pacc_rot.next()
                    for ct in range(4):
                        mm(pm_[:, 0:TW], wp[:, ct, dt_ * 128:(dt_ + 1) * 128], pooled[:, ct, hf * TW:(hf + 1) * TW],
                           ct == 0, ct == 3, [wp_b, pooled_b[ct]], [pm_b2])
                    stt(ypT[:, dt_, hf * TW:(hf + 1) * TW], pm_[:, 0:TW], pscol[:, g * 4 + dt_:g * 4 + dt_ + 1],
                        gps_all[:, dt_, hf * TW:(hf + 1) * TW], ALU.mult, ALU.mult,
                        [pm_b2, pscol_b, gps_all_b[dt_]], [ypT_b])
            dma("sp", ymd[g * 4:(g + 1) * 4, :, :].rearrange("j p c -> p j c"), ypT, reads=[ypT_b],
                writes=[ymd_b[g]])

        pre_out = [load_w(wout_d, 0, WB), load_w(wout_d, WB, WB)]
        P.barrier()
        cv = Carve()
        xa = []
        ra = []
        for i in range(2):
            xa.append((cv.f32(NTB * WB, (NTB, WB)), Buf("xa%d" % i)))
            ra.append((cv.f32(NTB * WB, (NTB, WB)), Buf("ra%d" % i)))
        assert cv.off <= 8192
        cv.off = 8192
        rfull = [(PH[:, 0:4096], Buf("rfull0")), (PH[:, 4096:8192], Buf("rfull1"))]
        gbc = (cv.f32(4096), Buf("gbc"))
        dg = (cv.f32(128), Buf("dg"))
        junk2 = (cv.bf16(256), Buf("junk2"))
        out_cb = [Buf("outcb%d" % i) for i in range(D // WB)]

        ymT = hT
        dma("sp", ymT[:], ymd.rearrange("m p c -> p m c"), reads=ymd_b, writes=hT_tb)
        for g4 in range(KT // 4):
            pgb, pgb_b = paux_rot.next()
            for j in range(4):
                kt = g4 * 4 + j
                tsc(dg[0], ident, gatec[:, kt:kt + 1], None, ALU.mult, None, [cm_b, modg_b], [dg[1]])
                mm(pgb[:, j * 128:(j + 1) * 128], ones_mat[:], dg[0], True, True, [ones_b, dg[1]], [pgb_b])
            vcopy(gbc[0][:, g4 * 512:(g4 + 1) * 512], pgb[:, :], [pgb_b], [gbc[1]])

        n_out_blk = D // WB
        row_own = own_tile0 * 128

        def xa_load(cb):
            dma("sp", xa[cb % 2][0],
                x_d[row_own:row_own + SEG, cb * WB:(cb + 1) * WB].rearrange("(t p) c -> p t c", p=128),
                writes=[xa[cb % 2][1]])

        pend = pre_out
        xa_load(0)
        for cb in range(n_out_blk):
            wo = pend.pop(0)
            if cb + 1 < n_out_blk:
                xa_load(cb + 1)
            xa_, xa_b = xa[cb % 2]
            ra_, ra_b = ra[cb % 2]
            for t in range(NTB):
                py, py_b = pacc_rot.next()
                for kt in range(KT):
                    mm(py[:, 0:WB], ymT[:, kt, t * 128:(t + 1) * 128], wo[0][:, kt, :],
                       kt == 0, kt == KT - 1, [hT_tb[t], wo[1]], [py_b])
                tt(ra_[:, t, :], py[:, 0:WB], gbc[0][:, cb * WB:(cb + 1) * WB], ALU.mult, [py_b, gbc[1]], [ra_b])
                tt(ra_[:, t, :], ra_[:, t, :], xa_[:, t, :], ALU.add, [ra_b, xa_b], [ra_b])
                act(junk2[0], ra_[:, t, :], AF.Square, [ra_b], [junk2[1], ssq_b],
                    accum_out=ssq[:, t * 16 + cb:t * 16 + cb + 1])
            dma("sp", out_d[:, cb * WB:(cb + 1) * WB].rearrange("(t p) c -> p t c", p=128), ra_,
                reads=[ra_b], writes=[out_cb[cb]])
            if cb + 2 < n_out_blk:
                pend.append(load_w(wout_d, (cb + 2) * WB, WB))

        P.op("dve", lambda e: e.tensor_reduce(out=fin[:, 0:NTB], in_=ssq[:].rearrange("p (t c) -> p t c", c=16),
                                              axis=AX.X, op=ALU.add), reads=[ssq_b], writes=[fin_b])
        act(fin[:, NTB:2 * NTB], fin[:, 0:NTB], AF.Sqrt, [fin_b, eps_b], [fin_b], bias=eps_t[:], scale=1.0 / D)
        recip(fin[:, 2 * NTB:3 * NTB], fin[:, NTB:2 * NTB], [fin_b], [fin_b])
        dma("sp", gbc[0], fnw_d[:, :], writes=[gbc[1]])
        P.barrier()
        def rf_load(t):
            if t < NTB:
                dma("sp", rfull[t % 2][0], out_d[t * 128:(t + 1) * 128, :], reads=out_cb, writes=[rfull[t % 2][1]])

        rf_load(0)
        rf_load(1)
        for t in range(NTB):
            rf, rf_b = rfull[t % 2]
            stt(rf, rf, fin[:, 2 * NTB + t:2 * NTB + t + 1], gbc[0], ALU.mult, ALU.mult,
                [rf_b, fin_b, gbc[1]], [rf_b])
            dma("sp", out_d[t * 128:(t + 1) * 128, :], rf, reads=[rf_b], writes=[out_b[t]])
            rf_load(t + 2)

        P.barrier()
        P.op("sp", lambda e: e.nop())
        import os
        if os.environ.get("KDBG"):
            print("sbuf remaining", nc.sbuf_bytes_remaining if not callable(nc.sbuf_bytes_remaining) else nc.sbuf_bytes_remaining())
        P.run()
    return nc


def _const_mats():
    j = np.arange(128)[:, None]
    i = np.arange(128)[None, :]
    ident = (j == i).astype(np.float32)
    tri = (j <= i).astype(np.float32)
    ustr = (j > i).astype(np.float32)
    return np.ascontiguousarray(np.concatenate([ident, tri, ustr], axis=1))


def _col(v):
    v = np.asarray(v, dtype=np.float32).reshape(-1, 128)
    return np.ascontiguousarray(v.T)


_NC_CACHE = {}


def kernel(x, c, w_ada, b_ada, norm_w, w_in, w_pool, pool_scale, w_alpha, b_alpha,
           gla_norm_w, w_out, final_norm_w):
    x = np.asarray(x, dtype=np.float32)
    B, SEQ, d = x.shape
    assert B == 1 and d == D and SEQ % (N_CORES * 128) == 0
    SEG = SEQ // N_CORES
    NT = SEQ // 128
    if SEQ not in _NC_CACHE:
        _NC_CACHE[SEQ] = build_program(SEQ)
    nc = _NC_CACHE[SEQ]

    x2 = x[0]
    shared = {
        "c_col": _col(np.asarray(c, np.float32)[0]),
        "w_ada": np.ascontiguousarray(np.asarray(w_ada, np.float32)[0]),
        "b_ada_col": _col(np.asarray(b_ada, np.float32)[0]),
        "nw_col": _col(np.asarray(norm_w, np.float32)[0]),
        "w_in": np.ascontiguousarray(np.asarray(w_in, np.float32)[0]),
        "w_pool": np.ascontiguousarray(np.asarray(w_pool, np.float32)[0]),
        "ps_col": _col(np.asarray(pool_scale, np.float32)[0]),
        "w_alpha_aug": np.ascontiguousarray(np.concatenate(
            [np.asarray(w_alpha, np.float32)[0], np.asarray(b_alpha, np.float32)[0][None, :]], axis=0)),
        "gnw_bc": np.ascontiguousarray(np.broadcast_to(np.asarray(gla_norm_w, np.float32)[0][None, :], (128, GLA_DV))),
        "w_out": np.ascontiguousarray(np.asarray(w_out, np.float32)[0]),
        "fnw_bc": np.ascontiguousarray(np.broadcast_to(np.asarray(final_norm_w, np.float32)[None, :], (128, D))),
        "cmat": _const_mats(),
    }
    in_maps = []
    for i in range(N_CORES):
        n_real = SEG * (i + 1)
        n_pad = SEQ - n_real
        xp = np.zeros((SEQ, D), np.float32)
        xp[n_pad:] = x2[:n_real]
        valid = np.zeros((SEQ,), np.float32)
        valid[n_pad:] = 1.0
        vmask = np.ascontiguousarray(valid.reshape(NT, 128).T)
        vh = valid[SEQ - SEG - 16:SEQ - SEG]
        vhalo = np.ascontiguousarray(np.broadcast_to(vh[None, :], (128, 16)))
        tg = SEG * i + np.arange(16)
        inv = np.stack([1.0 / np.minimum(tg + 1, w) for w in POOL_WINDOWS], axis=0).astype(np.float32)
        invcnt = np.ascontiguousarray(np.broadcast_to(inv.reshape(1, 64), (128, 64)))
        m = dict(shared)
        m.update({"x": xp, "vmask": vmask, "vhalo": vhalo, "invcnt": invcnt})
        in_maps.append(m)

    res = run_bass_kernel_spmd(nc, in_maps, core_ids=list(range(N_CORES)))
    outs = [np.asarray(r["out"], dtype=np.float32) for r in res.results]
    return np.concatenate(outs, axis=0).reshape(1, SEQ, D)
```
